# Optimizing a Trainium2 kernel written in Bass

```python
import jax, jax.numpy as jnp
from jax import lax
import numpy as np

D_MODEL = 1024
BATCH = 4
SEQ = 4096
DEPTH = 1

CONV_DIM = D_MODEL // 2
CONV_WIDTH = 3
RWKV_DIM = D_MODEL - CONV_DIM
HEAD_DIM = 64
RWKV_HEADS = RWKV_DIM // HEAD_DIM
DECAY_RANK = 64
ICLR_RANK = 64
GATE_RANK = 160
D_FF = 4 * D_MODEL
PLE_DIM = 256
CONV_COLS = 3 * CONV_DIM
RWKV_COLS = 3 * RWKV_DIM + DECAY_RANK + ICLR_RANK + GATE_RANK
IN_COLS = CONV_COLS + RWKV_COLS
RWKV_SPLITS = (RWKV_DIM, 2 * RWKV_DIM, 3 * RWKV_DIM, 3 * RWKV_DIM + DECAY_RANK, 3 * RWKV_DIM + DECAY_RANK + ICLR_RANK)
RMS_EPS = 1e-6
GN_EPS = 64e-5
L2_EPS = 1e-12

kernel_name = 'hybrid_shortconv_rwkv7_block'


def _rms_norm(h, g):
    hf = h.astype(jnp.float32)
    y = hf * lax.rsqrt(jnp.mean(hf * hf, axis=-1, keepdims=True) + RMS_EPS)
    return (y * g.astype(jnp.float32)).astype(h.dtype)


def _shift_one(u):
    return jnp.pad(u, ((0, 0), (1, 0), (0, 0)))[:, :-1]


def _short_gated_conv(cols, conv_w):
    gate_b, gate_c, hx = jnp.split(cols, 3, axis=-1)
    u = gate_c * hx
    t_len = u.shape[1]
    up = jnp.pad(u, ((0, 0), (CONV_WIDTH - 1, 0), (0, 0)))
    conv = up[:, 0:t_len] * conv_w[0]
    for j in range(1, CONV_WIDTH):
        conv = conv + up[:, j:j + t_len] * conv_w[j]
    return gate_b * conv


def _rwkv7_recurrence(r, w, k, v, a, b):
    bsz, _, n_heads, n = r.shape

    def step(s, inp):
        r_t, w_t, k_t, v_t, a_t, b_t = inp
        sa = jnp.einsum('bhvk,bhk->bhv', s, a_t)
        s = s * w_t[:, :, None, :] + sa[..., None] * b_t[:, :, None, :] + v_t[..., None] * k_t[:, :, None, :]
        return s, jnp.einsum('bhvk,bhk->bhv', s, r_t)

    xs = tuple(jnp.moveaxis(t, 1, 0) for t in (r, w, k, v, a, b))
    s0 = jnp.zeros((bsz, n_heads, n, n), jnp.float32)
    _, ys = lax.scan(step, s0, xs)
    return jnp.moveaxis(ys, 0, 1)


def _rwkv7_time_mix(cols, shift_mu, w_lora_up, w0, a_lora_up, a0, g_lora_up, k_k, k_a, r_k, ln_x_g, ln_x_b):
    f32 = jnp.float32
    bsz, t_len, _ = cols.shape
    u = cols + shift_mu * (_shift_one(cols) - cols)
    r, k, v, xw, xa, xg = jnp.split(u, RWKV_SPLITS, axis=-1)
    w_log = -jax.nn.softplus(-(w0 + jnp.tanh(xw) @ w_lora_up).astype(f32)) - 0.5
    decay = jnp.exp(-jnp.exp(w_log))
    iclr = jax.nn.sigmoid((a0 + xa @ a_lora_up).astype(f32))
    g = jax.nn.sigmoid(xg) @ g_lora_up

    def heads(t):
        return t.astype(f32).reshape(bsz, t_len, RWKV_HEADS, HEAD_DIM)

    kk = heads(k * k_k)
    kk = kk / jnp.maximum(jnp.sqrt(jnp.sum(kk * kk, axis=-1, keepdims=True)), L2_EPS)
    k_h = heads(k.astype(f32) * (1.0 + (iclr - 1.0) * k_a.astype(f32)))
    r_h, v_h, a_h = heads(r), heads(v), heads(iclr)
    y = _rwkv7_recurrence(r_h, heads(decay), k_h, v_h, -kk, kk * a_h)
    mu = jnp.mean(y, axis=-1, keepdims=True)
    var = jnp.mean(jnp.square(y - mu), axis=-1, keepdims=True)
    y = ((y - mu) * lax.rsqrt(var + GN_EPS)).reshape(bsz, t_len, RWKV_DIM)
    y = y * ln_x_g.astype(f32) + ln_x_b.astype(f32)
    bonus = jnp.sum(r_h * k_h * r_k.astype(f32), axis=-1, keepdims=True) * v_h
    y = (y + bonus.reshape(bsz, t_len, RWKV_DIM)) * g.astype(f32)
    return y.astype(cols.dtype)


def setup_inputs(seed: int = 0) -> dict:
    key = jax.random.key(seed)
    ks = jax.random.split(key, 26)
    f32 = jnp.float32
    L, D = DEPTH, D_MODEL

    def nrm(k, shape, scale):
        return jax.random.normal(k, shape, f32) * scale

    return {
        'x': nrm(ks[0], (BATCH, SEQ, D), 1.0),
        'p': nrm(ks[1], (DEPTH, BATCH, SEQ, PLE_DIM), 1.0),
        'norm_mix_g': 1.0 + nrm(ks[2], (L, D), 0.02),
        'w_in': nrm(ks[3], (L, D, IN_COLS), D ** -0.5),
        'conv_w': nrm(ks[4], (L, CONV_WIDTH, CONV_DIM), CONV_WIDTH ** -0.5),
        'shift_mu': jax.random.uniform(ks[5], (L, RWKV_COLS), f32),
        'w_lora_up': nrm(ks[6], (L, DECAY_RANK, RWKV_DIM), DECAY_RANK ** -0.5),
        'w0': jax.random.uniform(ks[7], (L, RWKV_DIM), f32, -4.0, 1.0),
        'a_lora_up': nrm(ks[8], (L, ICLR_RANK, RWKV_DIM), ICLR_RANK ** -0.5),
        'a0': nrm(ks[9], (L, RWKV_DIM), 0.1),
        'g_lora_up': nrm(ks[10], (L, GATE_RANK, RWKV_DIM), GATE_RANK ** -0.5),
        'k_k': 0.85 + nrm(ks[11], (L, RWKV_DIM), 0.02),
        'k_a': 1.0 + nrm(ks[12], (L, RWKV_DIM), 0.02),
        'r_k': nrm(ks[13], (L, RWKV_HEADS, HEAD_DIM), 0.1),
        'ln_x_g': 1.0 + nrm(ks[14], (L, RWKV_DIM), 0.02),
        'ln_x_b': nrm(ks[15], (L, RWKV_DIM), 0.02),
        'w_out': nrm(ks[16], (L, D, D), D ** -0.5),
        'norm_mlp_g': 1.0 + nrm(ks[17], (L, D), 0.02),
        'w_up': nrm(ks[18], (L, D, D_FF), D ** -0.5),
        'w_down': nrm(ks[19], (L, D_FF, D), D_FF ** -0.5),
        'norm_ple_g': 1.0 + nrm(ks[20], (L, D), 0.02),
        'w_ple_gate': nrm(ks[21], (L, D, D), D ** -0.5),
        'w_ple_proj': nrm(ks[22], (L, PLE_DIM, D), PLE_DIM ** -0.5),
        'norm_final_g': 1.0 + nrm(ks[23], (D,), 0.02),
    }


def reference(x, p, norm_mix_g, w_in, conv_w, shift_mu, w_lora_up, w0, a_lora_up, a0, g_lora_up, k_k, k_a, r_k, ln_x_g, ln_x_b, w_out, norm_mlp_g, w_up, w_down, norm_ple_g, w_ple_gate, w_ple_proj, norm_final_g):
    h = x
    for i in range(DEPTH):
        proj = _rms_norm(h, norm_mix_g[i]) @ w_in[i]
        y_conv = _short_gated_conv(proj[..., :CONV_COLS], conv_w[i])
        y_rwkv = _rwkv7_time_mix(proj[..., CONV_COLS:], shift_mu[i], w_lora_up[i], w0[i], a_lora_up[i], a0[i], g_lora_up[i], k_k[i], k_a[i], r_k[i], ln_x_g[i], ln_x_b[i])
        h = h + jnp.concatenate([y_conv, y_rwkv], axis=-1) @ w_out[i]
        hidden = jax.nn.relu(_rms_norm(h, norm_mlp_g[i]) @ w_up[i])
        h = h + jnp.square(hidden) @ w_down[i]
        gate = jax.nn.sigmoid(_rms_norm(h, norm_ple_g[i]) @ w_ple_gate[i])
        h = h + gate * (p[i] @ w_ple_proj[i])
    return _rms_norm(h, norm_final_g)
```

```python
import numpy as np
from contextlib import ExitStack
import concourse.bass as bass
import concourse.mybir as mybir
from concourse.bass_utils import run_bass_kernel_spmd

F32 = mybir.dt.float32
BF16 = mybir.dt.bfloat16
AF = mybir.ActivationFunctionType
ALU = mybir.AluOpType
AX = mybir.AxisListType

D = 1024
NTOK = 2048
TA = 256
NTA = 16
TB = 512
NTB = 4
C = 64
DEC = 0.6065306597126334
GN_EPS = 64e-5
RMS_EPS = 1e-6

GMIX, GMLP, GPLE, GFIN, MU, CW, W0, A0, KK, KA, RK, LG, LB, NP = 0, 8, 16, 24, 32, 47, 59, 63, 67, 71, 75, 79, 83, 87
CI, COB, CON, CML, CMT, CMI, CI8, NCC = 0, 128, 256, 384, 896, 1408, 1920, 2432


class Res:
    __slots__ = ("name", "lw", "rd")

    def __init__(self, name):
        self.name = name
        self.lw = {}
        self.rd = {}


class Sched:
    def __init__(self, nc, es):
        self.nc = nc
        self.es = es
        self.eng = {}
        hs = {"pe": nc.tensor, "dve": nc.vector, "act": nc.scalar, "pool": nc.gpsimd, "sp": nc.sync}
        for name, h in hs.items():
            sem = es.enter_context(nc.semaphore("s_" + name))
            self.eng[name] = dict(h=h, sem=sem, cnt=0, known={})
        self.ndma = 0
        self.dmatoks = []

    def _sem(self, k):
        return self.eng[k[1]]["sem"] if k[0] == "eng" else k[1]

    def op(self, e, fn, reads=(), writes=(), dma=False, inc=True, nowaw=False):
        E = self.eng[e]
        waits = {}
        me = ("eng", e)
        for r in reads:
            for k, v in r.lw.items():
                if v > waits.get(k, 0):
                    waits[k] = v
        allsame = e in ("dve", "act", "pool")
        for w in writes:
            if not nowaw:
                for k, v in w.lw.items():
                    if (k != me or allsame) and v > waits.get(k, 0):
                        waits[k] = v
            for k, v in w.rd.items():
                if (k != me or allsame) and v > waits.get(k, 0):
                    waits[k] = v
        kn = E["known"]
        for k, v in waits.items():
            if kn.get(k, 0) >= v:
                continue
            E["h"].wait_ge(self._sem(k), v)
            kn[k] = v
        inst = fn(E["h"])
        if dma:
            sem = self.es.enter_context(self.nc.semaphore("d%d" % self.ndma))
            self.ndma += 1
            inst.then_inc(sem, 16)
            tok = (("dma", sem), 16)
            self.dmatoks.append(tok)
        elif inc:
            E["cnt"] += 1
            inst.then_inc(E["sem"], 1)
            tok = (me, E["cnt"])
        else:
            tok = (me, E["cnt"] + 1)
        k, v = tok
        for r in reads:
            if v > r.rd.get(k, 0):
                r.rd[k] = v
        for w in writes:
            if v > w.lw.get(k, 0):
                w.lw[k] = v
        return inst

    def prewait(self, e, reads=(), writes=()):
        E = self.eng[e]
        waits = {}
        me = ("eng", e)
        for r in reads:
            for k, v in r.lw.items():
                if k != me and v > waits.get(k, 0):
                    waits[k] = v
        for w in writes:
            for k, v in list(w.lw.items()) + list(w.rd.items()):
                if k != me and v > waits.get(k, 0):
                    waits[k] = v
        kn = E["known"]
        for k, v in waits.items():
            if kn.get(k, 0) >= v:
                continue
            E["h"].wait_ge(self._sem(k), v)
            kn[k] = v

    def barrier(self):
        for e, E in self.eng.items():
            for o, O in self.eng.items():
                if o == e or O["cnt"] == 0:
                    continue
                k = ("eng", o)
                if E["known"].get(k, 0) < O["cnt"]:
                    E["h"].wait_ge(O["sem"], O["cnt"])
                    E["known"][k] = O["cnt"]
            for k, v in self.dmatoks:
                if E["known"].get(k, 0) < v:
                    E["h"].wait_ge(k[1], v)
                    E["known"][k] = v
        self.dmatoks = []


SAME_ENGINE_ALL = True


def build_program(debug=False):
    nc = bass.Bass("TRN2", target_bir_lowering=False)
    dt = nc.dram_tensor
    xT = dt("xT", [D, 2 * NTOK], F32, kind="ExternalInput").ap()
    pT = dt("pT", [256, NTOK], F32, kind="ExternalInput").ap()
    prm_d = dt("prm", [128, NP], F32, kind="ExternalInput").ap()
    cst_d = dt("cst", [128, NCC], F32, kind="ExternalInput").ap()
    smk_d = dt("smk", [128, 4 * TA], F32, kind="ExternalInput").ap()
    w_in_d = dt("w_in", [D, 3360], F32, kind="ExternalInput").ap()
    lora_d = dt("lora", [128, 512], F32, kind="ExternalInput").ap()
    glora_d = dt("glora", [160, 512], F32, kind="ExternalInput").ap()
    w_out_d = dt("w_out", [D, D], F32, kind="ExternalInput").ap()
    w_up_d = dt("w_up", [D, 4096], F32, kind="ExternalInput").ap()
    w_dn_d = dt("w_down", [4096, D], F32, kind="ExternalInput").ap()
    w_gt_d = dt("w_gate", [D, D], F32, kind="ExternalInput").ap()
    w_pp_d = dt("w_pproj", [256, D], F32, kind="ExternalInput").ap()
    outT = dt("outT", [D, NTOK], F32, kind="ExternalOutput").ap()
    if debug:
        dbg_yrw = dt("dbg_yrw", [128, 4 * NTOK], F32, kind="ExternalOutput").ap()
        dbg_h = dt("dbg_h", [128, 8 * NTOK], F32, kind="ExternalOutput").ap()

    with ExitStack() as es:
        S = Sched(nc, es)
        op = S.op

        def sbt(stack, name, shape, dtype):
            return stack.enter_context(nc.sbuf_tensor("sb_" + name, shape, dtype)), Res(name)

        bigt = [es.enter_context(nc.psum_tensor("pbig%d" % k, [128, 1024], F32)) for k in range(4)]
        bankR = [Res("bank%d" % j) for j in range(8)]
        singles = [(bigt[j // 2][:, (j % 2) * 512:(j % 2 + 1) * 512], bankR[j]) for j in range(8)]
        bigs = [(bigt[k], [bankR[2 * k], bankR[2 * k + 1]]) for k in range(4)]
        ring = [0, 0]
        ringset = [singles[0:5]]
        pybank = singles[7]

        def bank():
            rs = ringset[0]
            b = rs[ring[0] % len(rs)]
            ring[0] += 1
            return b

        ringc = [0]

        def cbank():
            b = singles[5 + ringc[0] % 2]
            ringc[0] += 1
            return b

        def bigbank():
            b = bigs[ring[1] % 4]
            ring[1] += 1
            return b

        def bf(pb):
            return pb[:, :].bitcast(BF16)

        prm, Rprm = sbt(es, "prm", [128, NP], F32)
        omm, Romm = sbt(es, "omm", [128, 15], F32)
        oka, Roka = sbt(es, "oka", [128, 4], F32)
        cst, Rcst = sbt(es, "cst", [128, 384], BF16)
        idf, Ridf = sbt(es, "idf", [128, 128], F32)
        yrw, Ryrw = sbt(es, "yrw", [128, 4, NTOK], BF16)
        epsb, Repsb = sbt(es, "epsb", [128, 4], F32)

        x2, Rx2 = sbt(es, "x2", [128, 8, 2], F32)
        op("sp", lambda h: h.dma_start(out=prm[:], in_=prm_d), writes=[Rprm], dma=True)
        op("sp", lambda h: h.dma_start(out=x2[:], in_=xT[:, NTOK - 2:NTOK].rearrange("(c p) t -> p c t", p=128)), writes=[Rx2], dma=True)
        op("sp", lambda h: h.dma_start(out=idf[:], in_=cst_d[:, 0:128]), writes=[Ridf], dma=True)
        op("pool", lambda h: h.dma_start(out=cst[:], in_=cst_d[:, 0:384], max_dma_last_dim=4096), writes=[Rcst], dma=True)
        op("dve", lambda h: h.tensor_scalar(out=omm[:], in0=prm[:, MU:MU + 15], scalar1=-1.0, scalar2=1.0, op0=ALU.mult, op1=ALU.add), reads=[Rprm], writes=[Romm])
        op("dve", lambda h: h.tensor_scalar(out=oka[:], in0=prm[:, KA:KA + 4], scalar1=-1.0, scalar2=1.0, op0=ALU.mult, op1=ALU.add), reads=[Rprm], writes=[Roka])
        op("dve", lambda h: h.memset(epsb[:, 0:1], RMS_EPS), writes=[Repsb])
        op("dve", lambda h: h.memset(epsb[:, 1:2], GN_EPS), writes=[Repsb])

        ident = cst[:, CI:CI + 128]
        onesblk = cst[:, COB:COB + 128]
        ones = cst[:, CON:CON + 128]

        def rmsnorm(bufs, src, dst, gcol, T, Rsrc, Rdst):
            sq, Rsq, lnv, Rlnv, rstd, Rrstd = bufs
            Rsrc_l = Rsrc if isinstance(Rsrc, list) else [Rsrc]
            for c in range(8):
                op("act", lambda h, c=c: h.activation(out=sq[:, c, 0:T], in_=src(c), func=AF.Square), reads=Rsrc_l, writes=[Rsq])
            pb, Rpb = bank()
            for c in range(8):
                op("pe", lambda h, c=c: h.matmul(pb[:, 0:T], lhsT=ones, rhs=sq[:, c, 0:T], start=(c == 0), stop=(c == 7)),
                   reads=[Rsq, Rcst], writes=[Rpb], inc=(c == 7))
            op("act", lambda h: h.activation(out=lnv[:, 0:T], in_=pb[:, 0:T], func=AF.Ln, bias=epsb[:, 0:1], scale=1.0 / D), reads=[Rpb, Repsb], writes=[Rlnv])
            op("act", lambda h: h.activation(out=rstd[:, 0:T], in_=lnv[:, 0:T], func=AF.Exp, scale=-0.5), reads=[Rlnv], writes=[Rrstd])
            for c in range(8):
                op("dve", lambda h, c=c: h.scalar_tensor_tensor(out=dst(c), in0=src(c), scalar=prm[:, gcol + c:gcol + c + 1], in1=rstd[:, 0:T], op0=ALU.mult, op1=ALU.mult),
                   reads=Rsrc_l + [Rrstd, Rprm], writes=[Rdst])

        def rms_sq(bufs, src, T, Rsrc):
            sq, Rsq, lnv, Rlnv, rstd, Rrstd = bufs
            Rsrc_l = Rsrc if isinstance(Rsrc, list) else [Rsrc]
            for c in range(8):
                op("act", lambda h, c=c: h.activation(out=sq[:, c, 0:T], in_=src(c), func=AF.Square), reads=Rsrc_l, writes=[Rsq])

        def rms_fin(bufs, src, dst, gcol, T, Rsrc, Rdst):
            sq, Rsq, lnv, Rlnv, rstd, Rrstd = bufs
            Rsrc_l = Rsrc if isinstance(Rsrc, list) else [Rsrc]
            pb, Rpb = bank()
            for c in range(8):
                op("pe", lambda h, c=c: h.matmul(pb[:, 0:T], lhsT=ones, rhs=sq[:, c, 0:T], start=(c == 0), stop=(c == 7)),
                   reads=[Rsq, Rcst], writes=[Rpb], inc=(c == 7))
            op("act", lambda h: h.activation(out=lnv[:, 0:T], in_=pb[:, 0:T], func=AF.Ln, bias=epsb[:, 0:1], scale=1.0 / D), reads=[Rpb, Repsb], writes=[Rlnv])
            op("act", lambda h: h.activation(out=rstd[:, 0:T], in_=lnv[:, 0:T], func=AF.Exp, scale=-0.5), reads=[Rlnv], writes=[Rrstd])
            for c in range(8):
                op("dve", lambda h, c=c: h.scalar_tensor_tensor(out=dst(c), in0=src(c), scalar=prm[:, gcol + c:gcol + c + 1], in1=rstd[:, 0:T], op0=ALU.mult, op1=ALU.mult),
                   reads=Rsrc_l + [Rrstd, Rprm], writes=[Rdst])

        def run(g):
            for _ in g:
                pass

        def interleave(a, b, na=1, nb=2):
            da = db = False
            while not (da and db):
                for _ in range(na):
                    if not da:
                        try:
                            next(a)
                        except StopIteration:
                            da = True
                for _ in range(nb):
                    if not db:
                        try:
                            next(b)
                        except StopIteration:
                            db = True

        wmlp, Rwconv = sbt(es, "wmlp", [128, 16384], BF16)
        wconv = wmlp[:, 0:8 * 1536].rearrange("p (c n) -> p c n", c=8)
        with ExitStack() as sa:
            WR = 1824
            w_in = wmlp[:, 0:8 * WR].rearrange("p (c n) -> p c n", c=8)
            Rw_in = Rwconv
            lora, Rlora = sbt(sa, "lora", [128, 512], BF16)
            glora, Rglora = sbt(sa, "glora", [128, 2, 512], BF16)
            msk, Rmsk = sbt(sa, "msk", [128, 2048], BF16)
            smk, Rsmk = sbt(sa, "smk", [128, 4 * TA], F32)
            for c in range(8):
                op("pool", lambda h, c=c: h.dma_start(out=w_in[:, c, :], in_=w_in_d[c * 128:(c + 1) * 128, 1536:3360], max_dma_last_dim=4096), writes=[Rw_in], dma=True, nowaw=True)
            op("pool", lambda h: h.dma_start(out=lora[:], in_=lora_d, max_dma_last_dim=4096), writes=[Rlora], dma=True)
            op("pool", lambda h: h.dma_start(out=glora[:, 0, :], in_=glora_d[0:128, :], max_dma_last_dim=4096), writes=[Rglora], dma=True)
            op("pool", lambda h: h.dma_start(out=glora[0:32, 1, :], in_=glora_d[128:160, :], max_dma_last_dim=4096), writes=[Rglora], dma=True, nowaw=True)
            op("pool", lambda h: h.dma_start(out=msk[:], in_=cst_d[:, 384:2432], max_dma_last_dim=4096), writes=[Rmsk], dma=True)
            op("sp", lambda h: h.dma_start(out=smk[:], in_=smk_d), writes=[Rsmk], dma=True)

            def mask2(k):
                return msk[:, k * 512:(k + 1) * 512].unsqueeze(1).broadcast_to([128, 2, 512])

            sq, Rsq = sbt(sa, "sq", [128, 8, TA], BF16)
            lnv, Rlnv = sbt(sa, "lnv", [128, TA], F32)
            rstd, Rrstd = sbt(sa, "rstd", [128, TA], F32)
            xn, Rxn = sbt(sa, "xn", [128, 8, TA], BF16)
            nbufs = (sq, Rsq, lnv, Rlnv, rstd, Rrstd)
            shs = [sbt(sa, "sh%d" % i, [128, TA + 1], F32) for i in range(3)]
            tmps = [sbt(sa, "tm%d" % i, [128, TA], F32) for i in range(3)]
            halo, Rhalo = sbt(sa, "halo", [128, 16], F32)
            ru, Rru = sbt(sa, "ru", [128, 4, TA], F32)
            ku, Rku = sbt(sa, "ku", [128, 4, TA], F32)
            vu, Rvu = sbt(sa, "vu", [128, 4, TA], F32)
            xwa, Rxwa = sbt(sa, "xwa", [128, TA], F32)
            xg, Rxg = sbt(sa, "xg", [128, 2, TA], F32)
            tw, Rtw = sbt(sa, "tw", [128, TA], BF16)
            sg, Rsg = sbt(sa, "sg", [128, 2, TA], BF16)
            t12, _ = sbt(sa, "t12", [128, 8, TA], F32)
            xt = t12
            t0_, _ = sbt(sa, "t0", [128, 4, TA], F32)
            t3_, _ = sbt(sa, "t3", [128, 4, TA], F32)
            t4_, _ = sbt(sa, "t4", [128, 4, TA], F32)
            T5b = [t0_[:], t12[:, 0:4, :], t12[:, 4:8, :], t3_[:], t4_[:]]
            T5r = [[Res("t%d_%d" % (k, hp)) for hp in range(2)] for k in range(5)]
            Rxt = T5r[1] + T5r[2]

            def half2(n, shape, dtype):
                t, _ = sbt(sa, n, shape, dtype)
                return t, [Res(n + "_0"), Res(n + "_1")]
            ksq, Rksq = half2("ksq", [128, 4, TA], BF16)
            rkb, Rrkb = half2("rkb", [128, 4, TA], BF16)
            egp, Regp = half2("egp", [128, 4, TA], F32)
            Atz, RAtz = half2("Atz", [128, 4, 2, TA], BF16)
            Rtz, RRtz = half2("Rtz", [128, 4, 2, TA], BF16)
            Ktc, RKtc = half2("Ktc", [128, 4, TA], BF16)
            Btc, RBtc = half2("Btc", [128, 4, TA], BF16)
            vch, Rvch = half2("vch", [128, 4, TA], BF16)
            gate2, _ = sbt(sa, "gate2", [128, 2, 4, TA], BF16)
            bonus2, _ = sbt(sa, "bonus2", [128, 2, 4, TA], BF16)
            Rgate2 = [[Res("gate%d_%d" % (a, b)) for b in range(2)] for a in range(2)]
            Rbonus2 = [[Res("bonus%d_%d" % (a, b)) for b in range(2)] for a in range(2)]
            gC, _ = sbt(sa, "gC", [128, 2, 4, 4], F32)
            RgC = [Res("gC0"), Res("gC1")]
            ych, Rych = sbt(sa, "ych", [128, 4, TA], F32)
            KBt = [sbt(sa, "KBt%d" % i, [128, 1024], BF16) for i in range(2)]
            Vt = [sbt(sa, "Vt%d" % i, [128, 512], BF16) for i in range(2)]
            Lb = [[sbt(sa, "Lb%d_%d" % (k, i), [128, 1024], BF16) for i in range(2)] for k in range(2)]
            Ltb = [[sbt(sa, "Ltb%d_%d" % (k, i), [128, 1024], BF16) for i in range(2)] for k in range(2)]
            Sb = [[sbt(sa, "Sb%d_%d" % (k, i), [128, 1024], BF16) for i in range(2)] for k in range(2)]
            AakTs = [sbt(sa, "AakT%d" % i, [128, 1024], BF16) for i in range(2)]
            ArbT = [sbt(sa, "ArbT%d" % i, [128, 1024], BF16) for i in range(2)]
            ArkT = [sbt(sa, "ArkT%d" % i, [128, 1024], BF16) for i in range(2)]
            Zsb = [sbt(sa, "Zsb%d" % i, [128, 512], F32) for i in range(2)]
            Xs, RXs = sbt(sa, "Xs", [128, 512], BF16)
            Ub, RUb = sbt(sa, "Ub", [128, 512], BF16)
            Hf, RHf = sbt(sa, "Hf", [128, 4, 128], F32)
            Hb, RHb = sbt(sa, "Hb", [128, 4, 128], BF16)
            ytok, Rytok = sbt(sa, "ytok", [128, 512], F32)
            ysq, Rysq = sbt(sa, "ysq", [128, 512], F32)
            ycn, Rycn = sbt(sa, "ycn", [128, 512], F32)
            st, Rst = sbt(sa, "st", [128, 6, 8], F32)

            op("pool", lambda h: h.memset(halo[:], 0.0), writes=[Rhalo])
            op("pool", lambda h: h.memset(Hf[:], 0.0), writes=[RHf])
            op("pool", lambda h: h.memset(Hb[:], 0.0), writes=[RHb])
            op("pool", lambda h: h.memset(Atz[:], 0.0), writes=RAtz)
            op("pool", lambda h: h.memset(Rtz[:], 0.0), writes=RRtz)
            op("pool", lambda h: h.memset(xg[:], 0.0), writes=[Rxg])
            op("pool", lambda h: h.memset(sg[:], 0.0), writes=[Rsg])
            op("pool", lambda h: h.memset(ru[:], 0.0), writes=[Rru])

            shc = [0]

            def proj_shift_multi(specs):
                pbs = [bank() for _ in specs]
                S.prewait("pe", reads=[Rw_in, Rxn], writes=[r for _, r in pbs])
                n = len(specs)
                for si, (lo, M, j, dst, Rdst) in enumerate(specs):
                    pb, Rpb = pbs[si]
                    for c in range(8):
                        op("pe", lambda h, c=c: h.matmul(pb[0:M, 0:TA], lhsT=w_in[:, c, lo:lo + M], rhs=xn[:, c, :], start=(c == 0), stop=(c == 7)),
                           reads=[Rw_in, Rxn], writes=[Rpb], inc=(c == 7 and si == n - 1))
                for si, (lo, M, j, dst, Rdst) in enumerate(specs):
                    pb, Rpb = pbs[si]
                    sh, Rsh = shs[shc[0] % 3]
                    tm, Rtm = tmps[shc[0] % 3]
                    shc[0] += 1
                    op("act", lambda h: h.activation(out=sh[0:M, 1:TA + 1], in_=pb[0:M, 0:TA], func=AF.Copy), reads=[Rpb], writes=[Rsh])
                    op("pool", lambda h: h.tensor_copy(out=sh[0:M, 0:1], in_=halo[0:M, j:j + 1]), reads=[Rhalo], writes=[Rsh])
                    op("pool", lambda h: h.tensor_scalar(out=tm[0:M, :], in0=sh[0:M, 0:TA], scalar1=prm[0:M, MU + j:MU + j + 1], scalar2=0.0, op0=ALU.mult, op1=ALU.add),
                       reads=[Rsh, Rprm], writes=[Rtm])
                    op("dve", lambda h: h.scalar_tensor_tensor(out=dst, in0=sh[0:M, 1:TA + 1], scalar=omm[0:M, j:j + 1], in1=tm[0:M, :], op0=ALU.mult, op1=ALU.add),
                       reads=[Rsh, Rtm, Romm], writes=[Rdst])
                    op("pool", lambda h: h.tensor_copy(out=halo[0:M, j:j + 1], in_=sh[0:M, TA:TA + 1]), reads=[Rsh], writes=[Rhalo])

            def gen_NP(i):
                full = i >= 7
                col0 = i * TA
                op("sp", lambda h: h.dma_start(out=xt[:], in_=xT[:, col0:col0 + TA].rearrange("(c p) t -> p c t", p=128)), writes=Rxt, dma=True)
                rmsnorm(nbufs, lambda c: xt[:, c, :], lambda c: xn[:, c, :], GMIX, TA, Rxt, Rxn)
                yield
                specs = [(1536, 128, 12, xwa[:], Rxwa)]
                if full:
                    specs.append((1664, 128, 13, xg[:, 0, :], Rxg))
                    specs.append((1792, 32, 14, xg[0:32, 1, :], Rxg))
                for p in range(4):
                    if full:
                        specs.append((p * 128, 128, p, ru[:, p, :], Rru))
                    specs.append((512 + p * 128, 128, 4 + p, ku[:, p, :], Rku))
                    specs.append((1024 + p * 128, 128, 8 + p, vu[:, p, :], Rvu))
                for k in range(0, len(specs), 3):
                    proj_shift_multi(specs[k:k + 3])
                    yield

            def flat(t):
                return t[:].rearrange("p a b -> p (a b)")

            def emit_Epre(i):
                own = i >= 8
                op("act", lambda h: h.activation(out=tw[0:64, :], in_=xwa[0:64, :], func=AF.Tanh), reads=[Rxwa], writes=[Rtw])
                op("act", lambda h: h.activation(out=tw[64:128, :], in_=xwa[64:128, :], func=AF.Copy), reads=[Rxwa], writes=[Rtw])
                if own:
                    op("act", lambda h: h.activation(out=sg[:, 0, :], in_=xg[:, 0, :], func=AF.Sigmoid), reads=[Rxg], writes=[Rsg])
                    op("act", lambda h: h.activation(out=sg[0:32, 1, :], in_=xg[0:32, 1, :], func=AF.Sigmoid), reads=[Rxg], writes=[Rsg])

            def gen_E(i, hp):
                own = i >= 8
                ps_ = slice(2 * hp, 2 * hp + 2)

                def hv(t):
                    return t[:, ps_, :].rearrange("p a b -> p (a b)")
                t0, t1, t2, t3, t4 = T5b
                Rt0, Rt1, Rt2, Rt3, Rt4 = [T5r[k][hp] for k in range(5)]
                par = i % 2
                gate = gate2[:, par]
                bonus = bonus2[:, par]
                Rk, Rr, Re, RA, RR, RKt_, RBt_, Rv = [x[hp] for x in (Rksq, Rrkb, Regp, RAtz, RRtz, RKtc, RBtc, Rvch)]
                Rg = Rgate2[par][hp]
                Rbo = Rbonus2[par][hp]
                lb = {}
                bw = bank()
                ba = bank()
                bg = bank() if own else None
                for qi, p in enumerate((2 * hp, 2 * hp + 1)):
                    lb[("w", p)] = (bw[0][:, qi * TA:(qi + 1) * TA], bw[1])
                    lb[("a", p)] = (ba[0][:, qi * TA:(qi + 1) * TA], ba[1])
                    if own:
                        lb[("g", p)] = (bg[0][:, qi * TA:(qi + 1) * TA], bg[1])
                S.prewait("pe", reads=[Rlora, Rtw, Rglora, Rsg], writes=[r for _, r in lb.values()])
                for p in (2 * hp, 2 * hp + 1):
                    pw, Rpw = lb[("w", p)]
                    pa_, Rpa_ = lb[("a", p)]
                    last = (p == 2 * hp + 1)
                    op("pe", lambda h: h.matmul(pw[:, 0:TA], lhsT=lora[0:64, p * 128:(p + 1) * 128], rhs=tw[0:64, :], start=True, stop=True), reads=[Rlora, Rtw], writes=[Rpw], inc=False)
                    op("pe", lambda h: h.matmul(pa_[:, 0:TA], lhsT=lora[64:128, p * 128:(p + 1) * 128], rhs=tw[64:128, :], start=True, stop=True), reads=[Rlora, Rtw], writes=[Rpa_], inc=(last and not own))
                    if own:
                        pg, Rpg = lb[("g", p)]
                        op("pe", lambda h: h.matmul(pg[:, 0:TA], lhsT=glora[:, 0, p * 128:(p + 1) * 128], rhs=sg[:, 0, :], start=True, stop=False), reads=[Rglora, Rsg], writes=[Rpg], inc=False)
                        op("pe", lambda h: h.matmul(pg[:, 0:TA], lhsT=glora[0:32, 1, p * 128:(p + 1) * 128], rhs=sg[0:32, 1, :], start=False, stop=True), reads=[Rglora, Rsg], writes=[Rpg], inc=last)
                for p in (2 * hp, 2 * hp + 1):
                    pw, Rpw = lb[("w", p)]
                    pa_, Rpa_ = lb[("a", p)]
                    op("act", lambda h: h.activation(out=t0[:, p, :], in_=pw[:, 0:TA], func=AF.Sigmoid, bias=prm[:, W0 + p:W0 + p + 1]), reads=[Rpw, Rprm], writes=[Rt0])
                    op("act", lambda h: h.activation(out=t1[:, p, :], in_=pa_[:, 0:TA], func=AF.Sigmoid, bias=prm[:, A0 + p:A0 + p + 1]), reads=[Rpa_, Rprm], writes=[Rt1])
                    if own:
                        pg, Rpg = lb[("g", p)]
                        op("act", lambda h: h.activation(out=gate[:, p, :], in_=pg[:, 0:TA], func=AF.Copy), reads=[Rpg], writes=[Rg])
                yield
                op("dve", lambda h: h.tensor_tensor_scan(out=hv(t2), data0=smk[:, 0:2 * TA], data1=hv(t0), initial=0.0, op0=ALU.mult, op1=ALU.add), reads=[Rsmk, Rt0], writes=[Rt2])
                yield
                op("pool", lambda h: h.tensor_tensor(out=hv(t3), in0=hv(t2), in1=hv(t0), op=ALU.subtract), reads=[Rt2, Rt0], writes=[Rt3])
                op("act", lambda h: h.activation(out=hv(egp), in_=hv(t2), func=AF.Exp, scale=-DEC), reads=[Rt2], writes=[Re])
                op("act", lambda h: h.activation(out=hv(t0), in_=hv(t2), func=AF.Exp, scale=DEC), reads=[Rt2], writes=[Rt0])
                yield
                op("act", lambda h: h.activation(out=hv(t3), in_=hv(t3), func=AF.Exp, scale=-DEC), reads=[Rt3], writes=[Rt3])
                op("act", lambda h: h.activation(out=hv(vch), in_=hv(vu), func=AF.Copy), reads=[Rvu], writes=[Rv])
                for p in (2 * hp, 2 * hp + 1):
                    op("pool", lambda h: h.tensor_scalar(out=t2[:, p, :], in0=ku[:, p, :], scalar1=prm[:, KK + p:KK + p + 1], scalar2=0.0, op0=ALU.mult, op1=ALU.add), reads=[Rku, Rprm], writes=[Rt2])
                    op("act", lambda h: h.activation(out=ksq[:, p, :], in_=ku[:, p, :], func=AF.Square, scale=prm[:, KK + p:KK + p + 1]), reads=[Rku, Rprm], writes=[Rk])
                yield
                pn, Rpn = bank()
                for q in range(2):
                    p = hp * 2 + q
                    op("pe", lambda h: h.matmul(pn[:, q * TA:(q + 1) * TA], lhsT=onesblk, rhs=ksq[:, p, :], start=True, stop=True), reads=[Rcst, Rk], writes=[Rpn], inc=(q == 1))
                op("dve", lambda h: h.tensor_scalar_max(out=hv(t4), in0=pn[:, :], scalar1=1e-18), reads=[Rpn], writes=[Rt4])
                yield
                op("act", lambda h: h.activation(out=hv(t4), in_=hv(t4), func=AF.Ln), reads=[Rt4], writes=[Rt4])
                op("act", lambda h: h.activation(out=hv(t4), in_=hv(t4), func=AF.Exp, scale=-0.5), reads=[Rt4], writes=[Rt4])
                yield
                op("dve", lambda h: h.tensor_tensor(out=hv(t2), in0=hv(t2), in1=hv(t4), op=ALU.mult), reads=[Rt2, Rt4], writes=[Rt2])
                for p in (2 * hp, 2 * hp + 1):
                    op("pool", lambda h: h.tensor_scalar(out=t4[:, p, :], in0=t1[:, p, :], scalar1=prm[:, KA + p:KA + p + 1], scalar2=oka[:, p:p + 1], op0=ALU.mult, op1=ALU.add), reads=[Rt1, Rprm, Roka], writes=[Rt4])
                yield
                op("dve", lambda h: h.tensor_tensor(out=hv(t4), in0=hv(ku), in1=hv(t4), op=ALU.mult), reads=[Rku, Rt4], writes=[Rt4])
                op("pool", lambda h: h.tensor_tensor(out=hv(t1), in0=hv(t2), in1=hv(t1), op=ALU.mult), reads=[Rt2, Rt1], writes=[Rt1])
                yield
                op("dve", lambda h: h.tensor_tensor(out=hv(Ktc), in0=hv(t4), in1=hv(t0), op=ALU.mult), reads=[Rt4, Rt0], writes=[RKt_])
                op("dve", lambda h: h.tensor_tensor(out=hv(Btc), in0=hv(t1), in1=hv(t0), op=ALU.mult), reads=[Rt1, Rt0], writes=[RBt_])
                yield
                if own:
                    for p in (2 * hp, 2 * hp + 1):
                        op("dve", lambda h: h.scalar_tensor_tensor(out=rkb[:, p, :], in0=ru[:, p, :], scalar=prm[:, RK + p:RK + p + 1], in1=t4[:, p, :], op0=ALU.mult, op1=ALU.mult), reads=[Rru, Rprm, Rt4], writes=[Rr])
                    pbn, Rpbn = bank()
                    for q in range(2):
                        p = hp * 2 + q
                        op("pe", lambda h: h.matmul(pbn[:, q * TA:(q + 1) * TA], lhsT=onesblk, rhs=rkb[:, p, :], start=True, stop=True), reads=[Rcst, Rr], writes=[Rpbn], inc=(q == 1))
                    op("dve", lambda h: h.tensor_tensor(out=hv(bonus), in0=hv(vu), in1=pbn[:, :], op=ALU.mult), reads=[Rvu, Rpbn], writes=[Rbo])
                    yield

            def emit_Efin(i):
                own = i >= 8
                par = i % 2
                t2, t3 = T5b[2], T5b[3]
                for hp in range(2):
                    ps_ = slice(2 * hp, 2 * hp + 2)
                    for e in range(2):
                        sl = slice(e * 64, (e + 1) * 64)
                        op("dve", lambda h: h.scalar_tensor_tensor(out=Atz[sl, ps_, e, :], in0=t2[sl, ps_, :], scalar=-1.0, in1=t3[sl, ps_, :], op0=ALU.mult, op1=ALU.mult),
                           reads=[T5r[2][hp], T5r[3][hp]], writes=[RAtz[hp]])
                        if own:
                            op("pool", lambda h: h.tensor_tensor(out=Rtz[sl, ps_, e, :], in0=ru[sl, ps_, :], in1=egp[sl, ps_, :], op=ALU.mult), reads=[Rru, Regp[hp]], writes=[RRtz[hp]])
                for c in range(4):
                    op("pool", lambda h: h.tensor_copy(out=gC[:, par, :, c:c + 1], in_=egp[:, :, c * 64 + 63:c * 64 + 64]), reads=Regp, writes=[RgC[par]])

            def gen_E2(i):
                emit_Epre(i)
                a_, b_ = gen_E(i, 0), gen_E(i, 1)
                da = db = False
                while not (da and db):
                    if not da:
                        try:
                            next(a_)
                        except StopIteration:
                            da = True
                    if not db:
                        try:
                            next(b_)
                        except StopIteration:
                            db = True
                    yield

            def gen_front(i):
                yield from gen_NP(i)
                yield from gen_E2(i)

            def gen_prep2(i):
                own = i >= 8
                BS = [slice(blk * 128, (blk + 1) * 128) for blk in range(2)]

                def v2(t):
                    return t[:].rearrange("p (a b) -> p a b", a=2)

                pk = [bank(), bank()]
                pv, Rpv = bank()
                S.prewait("pe", reads=RKtc + RBtc + Rvch + [Rcst], writes=[pk[0][1], pk[1][1], Rpv])
                pvv = bf(pv)
                for blk in range(2):
                    pkbv = bf(pk[blk][0])
                    for p in range(4):
                        op("pe", lambda h, p=p: h.transpose(out=pkbv[:, p * 128:(p + 1) * 128], in_=Ktc[:, p, BS[blk]], identity=ident), reads=RKtc + [Rcst], writes=[pk[blk][1]], inc=False)
                    for p in range(4):
                        op("pe", lambda h, p=p: h.transpose(out=pkbv[:, 512 + p * 128:512 + (p + 1) * 128], in_=Btc[:, p, BS[blk]], identity=ident), reads=RBtc + [Rcst], writes=[pk[blk][1]], inc=False)
                for blk in range(2):
                    for p in range(4):
                        op("pe", lambda h, p=p: h.transpose(out=pvv[:, blk * 512 + p * 128:blk * 512 + (p + 1) * 128], in_=vch[:, p, BS[blk]], identity=ident), reads=Rvch + [Rcst], writes=[Rpv], inc=(blk == 1 and p == 3))
                for blk in range(2):
                    kbt, Rkbt = KBt[blk]
                    op("act", lambda h: h.activation(out=kbt[:], in_=bf(pk[blk][0]), func=AF.Copy), reads=[pk[blk][1]], writes=[Rkbt])
                    vt, Rvt = Vt[blk]
                    op("dve", lambda h: h.tensor_copy(out=vt[:], in_=pvv[:, blk * 512:(blk + 1) * 512]), reads=[Rpv], writes=[Rvt])
                yield

                def amat2(lhs, Rl, rhs, Rr, lz, rz, outs, mk):
                    pbs = [bigbank(), bigbank()]
                    S.prewait("pe", reads=Rl + Rr, writes=pbs[0][1] + pbs[1][1])
                    for blk in range(2):
                        pb, Rpb = pbs[blk]
                        for hh in range(8):
                            p, e = hh // 2, hh % 2
                            l_ap = lhs[:, p, e, BS[blk]] if lz else lhs[:, p, BS[blk]]
                            r_ap = rhs[:, p, e, BS[blk]] if rz else rhs[:, p, BS[blk]]
                            op("pe", lambda h: h.matmul(pb[:, hh * 128:(hh + 1) * 128], lhsT=l_ap, rhs=r_ap, start=True, stop=True),
                               reads=Rl + Rr, writes=Rpb, inc=(blk == 1 and hh == 7))
                    for blk in range(2):
                        pb, Rpb = pbs[blk]
                        o, Ro = outs[blk]
                        op("dve", lambda h: h.tensor_tensor(out=v2(o), in0=v2(pb), in1=mask2(mk), op=ALU.mult), reads=Rpb + [Rmsk], writes=[Ro])

                amat2(Atz, RAtz, Btc, RBtc, True, False, [Lb[0][0], Lb[1][0]], 0)
                yield
                amat2(Btc, RBtc, Atz, RAtz, False, True, [Ltb[0][0], Ltb[1][0]], 1)
                for blk in range(2):
                    S0, RS0 = Sb[blk][0]
                    Lt0, RLt0 = Ltb[blk][0]
                    op("pool", lambda h: h.tensor_tensor(out=v2(S0), in0=v2(Lt0), in1=mask2(3), op=ALU.add), reads=[RLt0, Rmsk], writes=[RS0])
                yield
                amat2(Ktc, RKtc, Atz, RAtz, False, True, AakTs, 1)
                yield
                if own:
                    amat2(Btc, RBtc, Rtz, RRtz, False, True, ArbT, 2)
                    yield
                    amat2(Ktc, RKtc, Rtz, RRtz, False, True, ArkT, 2)
                    yield

                def hmm_burst(jobs):
                    pbs = [bigbank() for _ in jobs]
                    rr = []
                    ww = []
                    for (lhs, Rl, rhs, Rr), (pb, Rpb) in zip(jobs, pbs):
                        rr += [Rl, Rr]
                        ww += Rpb
                    S.prewait("pe", reads=rr, writes=ww)
                    for ji, ((lhs, Rl, rhs, Rr), (pb, Rpb)) in enumerate(zip(jobs, pbs)):
                        for hh in range(8):
                            cs = slice(hh * 128, (hh + 1) * 128)
                            op("pe", lambda h: h.matmul(pb[:, cs], lhsT=lhs[:, cs], rhs=rhs[:, cs], start=True, stop=True), reads=[Rl, Rr], writes=Rpb,
                               inc=(ji == len(jobs) - 1 and hh == 7))
                    return pbs

                cur = 0
                for lvl in range(5):
                    for blk in range(2):
                        Lc, RLc = Lb[blk][cur]
                        Ltc, RLtc = Ltb[blk][cur]
                        Ln_, RLn_ = Lb[blk][1 - cur]
                        Ltn, RLtn = Ltb[blk][1 - cur]
                        jobs = [(Ltc, RLtc, Lc, RLc)]
                        if lvl < 4:
                            jobs.append((Lc, RLc, Ltc, RLtc))
                        pbs = hmm_burst(jobs)
                        pb, Rpb = pbs[0]
                        op("act", lambda h: h.activation(out=Ln_[:], in_=pb[:, :], func=AF.Copy), reads=Rpb, writes=[RLn_])
                        if lvl < 4:
                            pb2, Rpb2 = pbs[1]
                            if blk == 0 or lvl % 2 == 0:
                                op("dve", lambda h: h.tensor_copy(out=Ltn[:], in_=pb2[:, :]), reads=Rpb2, writes=[RLtn])
                            else:
                                op("act", lambda h: h.activation(out=Ltn[:], in_=pb2[:, :], func=AF.Copy), reads=Rpb2, writes=[RLtn])
                        yield
                    for blk in range(2):
                        Ln_, RLn_ = Lb[blk][1 - cur]
                        Sc, RSc = Sb[blk][cur]
                        Sn, RSn = Sb[blk][1 - cur]
                        pbs = hmm_burst([(Ln_, RLn_, Sc, RSc)])
                        pb3, Rpb3 = pbs[0]
                        op("dve", lambda h: h.tensor_tensor(out=Sn[:], in0=Sc[:], in1=pb3[:, :], op=ALU.add), reads=[RSc] + Rpb3, writes=[RSn])
                        yield
                    cur = 1 - cur
                pz = [bank(), bank()]
                S.prewait("pe", reads=[AakTs[0][1], AakTs[1][1], Vt[0][1], Vt[1][1]], writes=[pz[0][1], pz[1][1]])
                for blk in range(2):
                    pb, Rpb = pz[blk]
                    AakT, RAakT = AakTs[blk]
                    vt, Rvt = Vt[blk]
                    for hh in range(8):
                        op("pe", lambda h: h.matmul(pb[:, hh * 64:(hh + 1) * 64], lhsT=AakT[:, hh * 128:(hh + 1) * 128], rhs=vt[:, hh * 64:(hh + 1) * 64], start=True, stop=True),
                           reads=[RAakT, Rvt], writes=[Rpb], inc=(blk == 1 and hh == 7))
                for blk in range(2):
                    pb, Rpb = pz[blk]
                    zs, Rzs = Zsb[blk]
                    op("act", lambda h: h.activation(out=zs[:], in_=pb[:, :], func=AF.Copy), reads=[Rpb], writes=[Rzs])
                yield

            def gen_chain(i, blk):
                own = i >= 8
                kbt, Rkbt = KBt[blk]
                vt, Rvt = Vt[blk]
                arb, Rarb = ArbT[blk]
                ark, Rark = ArkT[blk]
                zs, Rzs = Zsb[blk]
                TT, RTT = Sb[blk][1]
                if own:
                    py, Rpy = pybank
                for hf in range(2):
                    rows = slice(hf * 64, (hf + 1) * 64)
                    ts = slice(blk * 128 + hf * 64, blk * 128 + (hf + 1) * 64)
                    px, Rpx = cbank()
                    for hh in range(8):
                        p, e = hh // 2, hh % 2
                        op("pe", lambda h, hh=hh, p=p, e=e: h.matmul(px[rows, hh * 64:(hh + 1) * 64], lhsT=Atz[:, p, e, ts], rhs=Hb[:, p, e * 64:(e + 1) * 64], start=True, stop=True),
                           reads=RAtz + [RHb], writes=[Rpx], inc=(hh == 7))
                    op("dve", lambda h, px=px: h.tensor_tensor(out=Xs[rows, :], in0=zs[rows, :], in1=px[rows, :], op=ALU.add), reads=[Rzs, Rpx], writes=[RXs])
                    yield
                    pu, Rpu = cbank()
                    for hh in range(8):
                        op("pe", lambda h, hh=hh: h.matmul(pu[rows, hh * 64:(hh + 1) * 64], lhsT=TT[rows, hh * 128 + hf * 64:hh * 128 + (hf + 1) * 64], rhs=Xs[rows, hh * 64:(hh + 1) * 64], start=True, stop=True),
                           reads=[RTT, RXs], writes=[Rpu], inc=(hh == 7))
                    op("act", lambda h, pu=pu: h.activation(out=Ub[rows, :], in_=pu[rows, :], func=AF.Copy), reads=[Rpu], writes=[RUb])
                    yield
                    ph, Rph = cbank()
                    S.prewait("pe", reads=RRtz + [RHb, Rarb, Rark, RUb, Rvt, Rkbt], writes=[Rph] + ([Rpy] if own else []))
                    for p in range(4):
                        cs = slice(p * 128, (p + 1) * 128)
                        op("pe", lambda h, cs=cs: h.matmul(ph[:, cs], lhsT=kbt[rows, cs], rhs=vt[rows, cs], start=True, stop=False), reads=[Rkbt, Rvt], writes=[Rph], inc=False)
                        op("pe", lambda h, cs=cs, p=p: h.matmul(ph[:, cs], lhsT=kbt[rows, 512 + p * 128:512 + (p + 1) * 128], rhs=Ub[rows, cs], start=False, stop=True), reads=[Rkbt, RUb], writes=[Rph], inc=(p == 3))
                    if own:
                        for hh in range(8):
                            p, e = hh // 2, hh % 2
                            cs = slice(hh * 64, (hh + 1) * 64)
                            tcs = slice(hh * 128 + hf * 64, hh * 128 + (hf + 1) * 64)
                            op("pe", lambda h, p=p, e=e, cs=cs: h.matmul(py[rows, cs], lhsT=Rtz[:, p, e, ts], rhs=Hb[:, p, e * 64:(e + 1) * 64], start=True, stop=False),
                               reads=RRtz + [RHb], writes=[Rpy], inc=False)
                            op("pe", lambda h, cs=cs, tcs=tcs: h.matmul(py[rows, cs], lhsT=arb[rows, tcs], rhs=Ub[rows, cs], start=False, stop=False), reads=[Rarb, RUb], writes=[Rpy], inc=False)
                            op("pe", lambda h, cs=cs, tcs=tcs: h.matmul(py[rows, cs], lhsT=ark[rows, tcs], rhs=vt[rows, cs], start=False, stop=True), reads=[Rark, Rvt], writes=[Rpy], inc=(hh == 7))
                    op("dve", lambda h, ph=ph: h.tensor_tensor(out=flat(Hf), in0=flat(Hf), in1=ph[:, :], op=ALU.add), reads=[RHf, Rph], writes=[RHf])
                    cc = blk * 2 + hf
                    gcb = gC[:, i % 2, :, cc:cc + 1].broadcast_to([128, 4, 128])
                    op("pool", lambda h, gcb=gcb: h.tensor_tensor(out=Hf[:], in0=Hf[:], in1=gcb, op=ALU.mult), reads=[RHf, RgC[i % 2]], writes=[RHf])
                    op("act", lambda h: h.activation(out=Hb[:], in_=Hf[:], func=AF.Copy), reads=[RHf], writes=[RHb])
                    yield
                if own:
                    bs = slice(blk * 128, (blk + 1) * 128)
                    op("act", lambda h: h.activation(out=ytok[:], in_=py[:, :], func=AF.Copy), reads=[Rpy], writes=[Rytok])
                    y3 = ytok[:].rearrange("p (a b) -> p a b", a=8)
                    op("dve", lambda h: h.tensor_reduce(out=st[:, 0, :], in_=y3, axis=AX.X, op=ALU.add), reads=[Rytok], writes=[Rst])
                    op("act", lambda h: h.activation(out=ysq[:], in_=ytok[:], func=AF.Square), reads=[Rytok], writes=[Rysq])
                    op("dve", lambda h: h.tensor_reduce(out=st[:, 1, :], in_=ysq[:].rearrange("p (a b) -> p a b", a=8), axis=AX.X, op=ALU.add), reads=[Rysq], writes=[Rst])
                    op("dve", lambda h: h.tensor_scalar(out=st[:, 2, :], in0=st[:, 0, :], scalar1=1.0 / 64, scalar2=None, op0=ALU.mult), reads=[Rst], writes=[Rst])
                    op("dve", lambda h: h.tensor_tensor(out=st[:, 3, :], in0=st[:, 2, :], in1=st[:, 2, :], op=ALU.mult), reads=[Rst], writes=[Rst])
                    op("dve", lambda h: h.scalar_tensor_tensor(out=st[:, 4, :], in0=st[:, 1, :], scalar=1.0 / 64, in1=st[:, 3, :], op0=ALU.mult, op1=ALU.subtract), reads=[Rst], writes=[Rst])
                    op("act", lambda h: h.activation(out=st[:, 5, :], in_=st[:, 4, :], func=AF.Ln, bias=epsb[:, 1:2]), reads=[Rst, Repsb], writes=[Rst])
                    op("act", lambda h: h.activation(out=st[:, 5, :], in_=st[:, 5, :], func=AF.Exp, scale=-0.5), reads=[Rst], writes=[Rst])
                    yield
                    mb = st[:, 2, :].unsqueeze(2).broadcast_to([128, 8, 64])
                    rb = st[:, 5, :].unsqueeze(2).broadcast_to([128, 8, 64])
                    op("dve", lambda h: h.tensor_tensor(out=ycn[:].rearrange("p (a b) -> p a b", a=8), in0=y3, in1=mb, op=ALU.subtract), reads=[Rytok, Rst], writes=[Rycn])
                    op("dve", lambda h: h.tensor_tensor(out=ysq[:].rearrange("p (a b) -> p a b", a=8), in0=ycn[:].rearrange("p (a b) -> p a b", a=8), in1=rb, op=ALU.mult), reads=[Rycn, Rst], writes=[Rysq])
                    pyt, Rpyt = bank()
                    for p in range(4):
                        op("pe", lambda h, p=p: h.transpose(out=pyt[:, p * 128:(p + 1) * 128], in_=ysq[:, p * 128:(p + 1) * 128], identity=idf[:]), reads=[Rysq, Ridf], writes=[Rpyt], inc=(p == 3))
                    op("act", lambda h: h.activation(out=ych[:, :, bs], in_=pyt[:, :].rearrange("p (a b) -> p a b", a=4), func=AF.Copy), reads=[Rpyt], writes=[Rych])
                    yield

            def emit_F(i):
                oc0 = (i - 8) * TA
                for p in range(4):
                    tm, Rtm = tmps[p % 3]
                    op("pool", lambda h, p=p, tm=tm: h.tensor_scalar(out=tm[:], in0=ych[:, p, :], scalar1=prm[:, LG + p:LG + p + 1], scalar2=prm[:, LB + p:LB + p + 1], op0=ALU.mult, op1=ALU.add),
                       reads=[Rych, Rprm], writes=[Rtm])
                    op("pool", lambda h, p=p, tm=tm: h.tensor_tensor(out=tm[:], in0=tm[:], in1=bonus2[:, i % 2, p, :], op=ALU.add), reads=[Rtm] + Rbonus2[i % 2], writes=[Rtm])
                    op("dve", lambda h, p=p, tm=tm: h.tensor_tensor(out=yrw[:, p, oc0:oc0 + TA], in0=tm[:], in1=gate2[:, i % 2, p, :], op=ALU.mult), reads=[Rtm] + Rgate2[i % 2], writes=[Ryrw])

            def empty():
                return
                yield

            run(gen_front(0))
            for i in range(NTA):
                emit_Efin(i)
                if i == NTA - 1:
                    for c in range(8):
                        op("pool", lambda h, c=c: h.dma_start(out=wconv[:, c, :], in_=w_in_d[c * 128:(c + 1) * 128, 0:1536], max_dma_last_dim=4096), writes=[Rwconv], dma=True, nowaw=True)
                run(gen_prep2(i))
                nxt = gen_front(i + 1) if i + 1 < NTA else empty()

                def both(i=i):
                    yield from gen_chain(i, 0)
                    yield from gen_chain(i, 1)
                interleave(both(), nxt, 2, 1)
                if i >= 8:
                    emit_F(i)
            S.barrier()
            if debug:
                op("pool", lambda h: h.dma_start(out=dbg_yrw, in_=yrw[:].rearrange("p a b -> p (a b)"), max_dma_last_dim=4096), reads=[Ryrw], dma=True)

        ringset[0] = singles
        with ExitStack() as sb_:
            hres, _ = sbt(sb_, "hres", [128, 8, NTOK], F32)
            Rht = [Res("hres%d" % t) for t in range(NTB)]
            big32, Rxnm = sbt(sb_, "big32", [128, 8 * NTOK], BF16)
            xnm = big32[:, :].rearrange("p (c t) -> p c t", c=8)
            obuf = big32[:, 0:8 * TB * 2].bitcast(F32).rearrange("p (c t) -> p c t", c=8)
            Robuf = Rxnm
            sq, Rsq = sbt(sb_, "sq2", [128, 8, TB], BF16)
            lnv, Rlnv = sbt(sb_, "lnv2", [128, TB], F32)
            rstd, Rrstd = sbt(sb_, "rstd2", [128, TB], F32)
            nbufs = (sq, Rsq, lnv, Rlnv, rstd, Rrstd)
            wu = [(wmlp[:, i * 4096:(i + 1) * 4096].rearrange("p (c n) -> p c n", c=8), Res("wu%d" % i)) for i in range(2)]
            wd = [(wmlp[:, 8192 + i * 4096:8192 + (i + 1) * 4096].rearrange("p (c n) -> p c n", c=4), Res("wd%d" % i)) for i in range(2)]
            hid, Rhid = sbt(sb_, "hid", [128, 8, TB], BF16)
            wsq = sbt(sb_, "wsq", [128, 8, 1024], BF16)
            wo, Rwo = wsq
            op("pool", lambda h: h.dma_start(out=wo[:], in_=w_out_d.rearrange("(c p) n -> p c n", p=128), max_dma_last_dim=4096), writes=[Rwo], dma=True)
            for t in range(NTB):
                op("sp", lambda h, t=t: h.dma_start(out=hres[:, :, t * TB:(t + 1) * TB], in_=xT[:, NTOK + t * TB:NTOK + (t + 1) * TB].rearrange("(c p) t -> p c t", p=128)), writes=[Rht[t]], dma=True)

            with ExitStack() as s2:
                Cs = [sbt(s2, "Csb%d" % i, [128, TB], F32) for i in range(2)]
                Bs = [sbt(s2, "Bsb%d" % i, [128, TB], F32) for i in range(2)]
                ubuf, Rubuf = sbt(s2, "ubuf", [128, 4, TB + 2], BF16)
                dg, Rdg = sbt(s2, "dg", [128, 12, 128], BF16)
                ycv, Rycv = sbt(s2, "ycv", [128, 4, TB], BF16)
                for j in range(3):
                    for q in range(4):
                        op("dve", lambda h, j=j, q=q: h.tensor_scalar(out=dg[:, j * 4 + q, :], in0=ident, scalar1=prm[:, CW + j * 4 + q:CW + j * 4 + q + 1], scalar2=None, op0=ALU.mult),
                           reads=[Rcst, Rprm], writes=[Rdg])

                def cproj(lo, T):
                    pb, Rpb = bank()
                    for c in range(8):
                        op("pe", lambda h, c=c: h.matmul(pb[:, 0:T], lhsT=wconv[:, c, lo:lo + 128], rhs=hid[:, c, 0:T], start=(c == 0), stop=(c == 7)),
                           reads=[Rwconv, Rhid], writes=[Rpb], inc=(c == 7))
                    return pb, Rpb

                rmsnorm(nbufs, lambda c: x2[:, c, :], lambda c: hid[:, c, 0:2], GMIX, 2, Rx2, Rhid)
                for q in range(4):
                    Csb, RCsb = Cs[q % 2]
                    pb, Rpb = cproj(512 + q * 128, 2)
                    op("act", lambda h, pb=pb, Csb=Csb: h.activation(out=Csb[:, 0:2], in_=pb[:, 0:2], func=AF.Copy), reads=[Rpb], writes=[RCsb])
                    pb2, Rpb2 = cproj(1024 + q * 128, 2)
                    op("dve", lambda h, q=q, pb2=pb2, Csb=Csb: h.tensor_tensor(out=ubuf[:, q, 0:2], in0=Csb[:, 0:2], in1=pb2[:, 0:2], op=ALU.mult), reads=[RCsb, Rpb2], writes=[Rubuf])
                rmsnorm(nbufs, lambda c: hres[:, c, 0:TB], lambda c: hid[:, c, :], GMIX, TB, Rht[0], Rhid)
                for t in range(NTB):
                    cs = slice(t * TB, (t + 1) * TB)
                    Rh = Rht[t]

                    def conv_tail(q):
                        Bsb, RBsb = Bs[q % 2]
                        pb4, Rpb4 = bank()
                        for j in range(3):
                            op("pe", lambda h, j=j: h.matmul(pb4[:, :], lhsT=dg[:, j * 4 + q, :], rhs=ubuf[:, q, j:j + TB], start=(j == 0), stop=(j == 2)),
                               reads=[Rdg, Rubuf], writes=[Rpb4], inc=(j == 2))
                        op("dve", lambda h: h.tensor_tensor(out=ycv[:, q, :], in0=Bsb[:], in1=pb4[:, :], op=ALU.mult), reads=[RBsb, Rpb4], writes=[Rycv])
                        op("pool", lambda h: h.tensor_copy(out=ubuf[:, q, 0:2], in_=ubuf[:, q, TB:TB + 2]), reads=[Rubuf], writes=[Rubuf])

                    for q in range(4):
                        Csb, RCsb = Cs[q % 2]
                        Bsb, RBsb = Bs[q % 2]
                        pb, Rpb = cproj(512 + q * 128, TB)
                        op("act", lambda h, pb=pb, Csb=Csb: h.activation(out=Csb[:], in_=pb[:, :], func=AF.Copy), reads=[Rpb], writes=[RCsb])
                        pb2, Rpb2 = cproj(1024 + q * 128, TB)
                        op("dve", lambda h, q=q, pb2=pb2, Csb=Csb: h.tensor_tensor(out=ubuf[:, q, 2:TB + 2], in0=Csb[:], in1=pb2[:, :], op=ALU.mult), reads=[RCsb, Rpb2], writes=[Rubuf])
                        pb3, Rpb3 = cproj(q * 128, TB)
                        op("act", lambda h, pb3=pb3, Bsb=Bsb: h.activation(out=Bsb[:], in_=pb3[:, :], func=AF.Copy), reads=[Rpb3], writes=[RBsb])
                        if q >= 1:
                            conv_tail(q - 1)
                    conv_tail(3)
                    ncs = slice((t + 1) * TB, (t + 2) * TB)
                    pcs = slice((t - 1) * TB, t * TB)
                    if t + 1 < NTB:
                        rms_sq(nbufs, lambda c, ncs=ncs: hres[:, c, ncs], TB, Rht[t + 1])
                    for dc in range(8):
                        if dc == 3 and t + 1 < NTB:
                            rms_fin(nbufs, lambda c, ncs=ncs: hres[:, c, ncs], lambda c: hid[:, c, :], GMIX, TB, Rht[t + 1], Rhid)
                        if dc == 4 and t >= 1:
                            rms_sq(nbufs, lambda c, pcs=pcs: hres[:, c, pcs], TB, Rht[t - 1])
                        if dc == 7 and t >= 1:
                            rms_fin(nbufs, lambda c, pcs=pcs: hres[:, c, pcs], lambda c, pcs=pcs: xnm[:, c, pcs], GMLP, TB, Rht[t - 1], Rxnm)
                        pb, Rpb = bank()
                        for c in range(8):
                            r_ap = ycv[:, c, :] if c < 4 else yrw[:, c - 4, cs]
                            op("pe", lambda h, c=c, dc=dc, pb=pb, r_ap=r_ap: h.matmul(pb[:, :], lhsT=wo[:, c, dc * 128:(dc + 1) * 128], rhs=r_ap, start=(c == 0), stop=(c == 7)),
                               reads=[Rwo, Rycv, Ryrw], writes=[Rpb], inc=(c == 7))
                        op("dve", lambda h, dc=dc, cs=cs, pb=pb: h.tensor_tensor(out=hres[:, dc, cs], in0=hres[:, dc, cs], in1=pb[:, :], op=ALU.add), reads=[Rh, Rpb], writes=[Rh])
                S.barrier()
                if debug:
                    op("sp", lambda h: h.dma_start(out=dbg_h, in_=hres[:].rearrange("p a b -> p (a b)")), reads=Rht, dma=True)
                    S.barrier()
            rls = [sbt(sb_, "rl%d" % i, [128, TB], F32) for i in range(3)]
            rlc = [0]

            def nextrl():
                r = rls[rlc[0] % 3]
                rlc[0] += 1
                return r
            wpp, Rwpp = sbt(sb_, "wpp", [128, 2, 1024], BF16)
            pbfs = [sbt(sb_, "pbf%d" % i, [128, 2, TB], BF16) for i in range(2)]
            op("pool", lambda h: h.dma_start(out=wpp[:], in_=w_pp_d.rearrange("(c p) n -> p c n", p=128), max_dma_last_dim=4096), writes=[Rwpp], dma=True)
            for t in range(NTB - 1, NTB):
                cs = slice(t * TB, (t + 1) * TB)
                rmsnorm(nbufs, lambda c, cs=cs: hres[:, c, cs], lambda c, cs=cs: xnm[:, c, cs], GMLP, TB, Rht[t], Rxnm)

            def load_q(q):
                w1, Rw1 = wu[q % 2]
                w2, Rw2 = wd[q % 2]
                op("pool", lambda h: h.dma_start(out=w1, in_=w_up_d[:, q * 512:(q + 1) * 512].rearrange("(c p) n -> p c n", p=128), max_dma_last_dim=2048), writes=[Rw1], dma=True)
                op("pool", lambda h: h.dma_start(out=w2, in_=w_dn_d[q * 512:(q + 1) * 512, :].rearrange("(c p) n -> p c n", p=128), max_dma_last_dim=4096), writes=[Rw2], dma=True)

            load_q(0)
            wg, Rwg = wsq
            op("pool", lambda h: h.dma_start(out=wg[:], in_=w_gt_d.rearrange("(c p) n -> p c n", p=128), max_dma_last_dim=4096), writes=[Rwg], dma=True)
            for t in range(2):
                op("pool", lambda h, t=t: h.dma_start(out=pbfs[t][0][:], in_=pT[:, t * TB:(t + 1) * TB].rearrange("(c p) t -> p c t", p=128), max_dma_last_dim=2048), writes=[pbfs[t][1]], dma=True)
            load_q(1)
            iters = [(q, t) for q in range(8) for t in range(NTB)]
            hbuf = [(hid[:, 0:4, :], Res("hidA")), (hid[:, 4:8, :], Res("hidB2"))]

            def emit_up(k):
                q, t = iters[k]
                w1, Rw1 = wu[q % 2]
                hb, Rhb_ = hbuf[k % 2]
                cs = slice(t * TB, (t + 1) * TB)
                for fc in range(4):
                    pb, Rpb = bank()
                    rl, Rrl = nextrl()
                    for c in range(8):
                        op("pe", lambda h, c=c, fc=fc, cs=cs, pb=pb: h.matmul(pb[:, :], lhsT=w1[:, c, fc * 128:(fc + 1) * 128], rhs=xnm[:, c, cs], start=(c == 0), stop=(c == 7)),
                           reads=[Rw1, Rxnm], writes=[Rpb], inc=(c == 7))
                    op("act", lambda h, pb=pb, rl=rl: h.activation(out=rl[:], in_=pb[:, :], func=AF.Relu), reads=[Rpb], writes=[Rrl])
                    op("dve", lambda h, fc=fc, pb=pb, rl=rl: h.tensor_tensor(out=hb[:, fc, :], in0=rl[:], in1=pb[:, :], op=ALU.mult), reads=[Rrl, Rpb], writes=[Rhb_])

            def emit_down(k):
                q, t = iters[k]
                w2, Rw2 = wd[q % 2]
                hb, Rhb_ = hbuf[k % 2]
                cs = slice(t * TB, (t + 1) * TB)
                for dc in range(8):
                    pb, Rpb = bank()
                    for fc in range(4):
                        op("pe", lambda h, fc=fc, dc=dc, pb=pb: h.matmul(pb[:, :], lhsT=w2[:, fc, dc * 128:(dc + 1) * 128], rhs=hb[:, fc, :], start=(fc == 0), stop=(fc == 3)),
                           reads=[Rw2, Rhb_], writes=[Rpb], inc=(fc == 3))
                    op("dve", lambda h, dc=dc, cs=cs, pb=pb: h.tensor_tensor(out=hres[:, dc, cs], in0=hres[:, dc, cs], in1=pb[:, :], op=ALU.add), reads=[Rht[t], Rpb], writes=[Rht[t]])

            emit_up(0)
            for k in range(1, len(iters)):
                emit_up(k)
                emit_down(k - 1)
                q, t = iters[k]
                if t == 0 and q >= 1 and q + 1 < 8:
                    load_q(q + 1)
            emit_down(len(iters) - 1)
            S.barrier()
            obuf = big32[:, 0:8192].bitcast(F32).rearrange("p (c t) -> p c t", c=8)
            Robuf = Res("obuf")
            hidB = big32[:, 8192:12288].rearrange("p (c t) -> p c t", c=8)
            hids = [(hid[:], Rhid), (hidB, Res("hidB"))]
            sqF = big32[:, 12288:16384].rearrange("p (c t) -> p c t", c=8)
            lnvF, RlnvF = sbt(sb_, "lnvF", [128, TB], F32)
            rstdF, RrstdF = sbt(sb_, "rstdF", [128, TB], F32)
            nbufsF = (sqF, Res("sqF"), lnvF, RlnvF, rstdF, RrstdF)

            def ple_norm_a(t):
                pbf, Rpbf = pbfs[t % 2]
                if t >= 2:
                    op("pool", lambda h: h.dma_start(out=pbf[:], in_=pT[:, t * TB:(t + 1) * TB].rearrange("(c p) t -> p c t", p=128), max_dma_last_dim=2048), writes=[Rpbf], dma=True)
                rms_sq(nbufs, lambda c: hres[:, c, t * TB:(t + 1) * TB], TB, Rht[t])

            def ple_norm_b(t):
                hd, Rhd = hids[t % 2]
                rms_fin(nbufs, lambda c: hres[:, c, t * TB:(t + 1) * TB], lambda c: hd[:, c, :], GPLE, TB, Rht[t], Rhd)

            def fin_a(t):
                rms_sq(nbufsF, lambda c: hres[:, c, t * TB:(t + 1) * TB], TB, Rht[t])

            def fin_b(t):
                tcs = slice(t * TB, (t + 1) * TB)
                rms_fin(nbufsF, lambda c: hres[:, c, tcs], lambda c: obuf[:, c, :], GFIN, TB, Rht[t], Robuf)
                op("sp", lambda h: h.dma_start(out=outT[:, tcs].rearrange("(c p) t -> p c t", p=128), in_=obuf), reads=[Robuf], writes=[], dma=True)

            ple_norm_a(0)
            ple_norm_b(0)
            for t in range(NTB):
                cs = slice(t * TB, (t + 1) * TB)
                pbf, Rpbf = pbfs[t % 2]
                hd, Rhd = hids[t % 2]
                for dc in range(8):
                    rl, Rrl = nextrl()
                    pb, Rpb = bank()
                    for c in range(8):
                        op("pe", lambda h, c=c, dc=dc, pb=pb: h.matmul(pb[:, :], lhsT=wg[:, c, dc * 128:(dc + 1) * 128], rhs=hd[:, c, :], start=(c == 0), stop=(c == 7)),
                           reads=[Rwg, Rhd], writes=[Rpb], inc=(c == 7))
                    op("act", lambda h, pb=pb, rl=rl: h.activation(out=rl[:], in_=pb[:, :], func=AF.Sigmoid), reads=[Rpb], writes=[Rrl])
                    pb2, Rpb2 = bank()
                    for c in range(2):
                        op("pe", lambda h, c=c, dc=dc, pb2=pb2, pbf=pbf: h.matmul(pb2[:, :], lhsT=wpp[:, c, dc * 128:(dc + 1) * 128], rhs=pbf[:, c, :], start=(c == 0), stop=(c == 1)),
                           reads=[Rwpp, Rpbf], writes=[Rpb2], inc=(c == 1))
                    op("dve", lambda h, pb2=pb2, rl=rl: h.tensor_tensor(out=rl[:], in0=rl[:], in1=pb2[:, :], op=ALU.mult), reads=[Rrl, Rpb2], writes=[Rrl])
                    eng = "pool" if dc % 2 == 0 else "dve"
                    op(eng, lambda h, dc=dc, cs=cs, rl=rl: h.tensor_tensor(out=hres[:, dc, cs], in0=hres[:, dc, cs], in1=rl[:], op=ALU.add), reads=[Rht[t], Rrl], writes=[Rht[t]])
                    if dc == 0 and t + 1 < NTB:
                        ple_norm_a(t + 1)
                    if dc == 2 and t + 1 < NTB:
                        ple_norm_b(t + 1)
                    if dc == 3 and t >= 1:
                        fin_a(t - 1)
                    if dc == 5 and t >= 1:
                        fin_b(t - 1)
            fin_a(NTB - 1)
            fin_b(NTB - 1)
            S.barrier()
    return nc


_NC = [None]


def _host_consts():
    cst = np.zeros((128, NCC), np.float32)
    cst[:, CI:CI + 128] = np.eye(128, dtype=np.float32)
    ob = np.zeros((128, 128), np.float32)
    ob[0:64, 0:64] = 1.0
    ob[64:128, 64:128] = 1.0
    cst[:, COB:COB + 128] = ob
    cst[:, CON:CON + 128] = 1.0
    def bd(m):
        o = np.zeros((128, 128), np.float32)
        o[0:64, 0:64] = m
        o[64:128, 64:128] = m
        return o
    o64 = np.ones((64, 64), np.float32)
    cst[:, CML:CML + 512] = np.tile(bd(np.tril(o64, -1)), (1, 4))
    cst[:, CMT:CMT + 512] = np.tile(bd(np.triu(o64, 1)), (1, 4))
    cst[:, CMI:CMI + 512] = np.tile(bd(np.triu(o64, 0)), (1, 4))
    cst[:, CI8:CI8 + 512] = np.tile(np.eye(128, dtype=np.float32), (1, 4))
    smk = np.ones((128, 4 * TA), np.float32)
    smk[:, ::C] = 0.0
    return cst, smk


def kernel(x, p, norm_mix_g, w_in, conv_w, shift_mu, w_lora_up, w0, a_lora_up, a0, g_lora_up, k_k, k_a, r_k,
           ln_x_g, ln_x_b, w_out, norm_mlp_g, w_up, w_down, norm_ple_g, w_ple_gate, w_ple_proj, norm_final_g):
    f = lambda a: np.ascontiguousarray(np.asarray(a, dtype=np.float32))
    x = f(x)
    p = f(p)

    def cols(v, n):
        return f(v).reshape(n, 128).T

    prm = np.zeros((128, NP), np.float32)
    prm[:, GMIX:GMIX + 8] = cols(norm_mix_g[0], 8)
    prm[:, GMLP:GMLP + 8] = cols(norm_mlp_g[0], 8)
    prm[:, GPLE:GPLE + 8] = cols(norm_ple_g[0], 8)
    prm[:, GFIN:GFIN + 8] = cols(norm_final_g, 8)
    mu = f(shift_mu[0])
    prm[:, MU:MU + 13] = cols(mu[0:1664], 13)
    prm[:, MU + 13] = mu[1664:1792]
    prm[0:32, MU + 14] = mu[1792:1824]
    cw = f(conv_w[0])
    for j in range(3):
        prm[:, CW + j * 4:CW + j * 4 + 4] = cols(cw[j], 4)
    prm[:, W0:W0 + 4] = cols(w0[0], 4)
    prm[:, A0:A0 + 4] = cols(a0[0], 4)
    prm[:, KK:KK + 4] = cols(k_k[0], 4)
    prm[:, KA:KA + 4] = cols(k_a[0], 4)
    prm[:, RK:RK + 4] = cols(f(r_k[0]).reshape(-1), 4)
    prm[:, LG:LG + 4] = cols(ln_x_g[0], 4)
    prm[:, LB:LB + 4] = cols(ln_x_b[0], 4)
    cst, smk = _host_consts()
    lora = np.concatenate([f(w_lora_up[0]), f(a_lora_up[0])], axis=0)
    shared = {
        "prm": prm, "cst": cst, "smk": smk, "w_in": f(w_in[0]), "lora": f(lora), "glora": f(g_lora_up[0]),
        "w_out": f(w_out[0]), "w_up": f(w_up[0]), "w_down": f(w_down[0]), "w_gate": f(w_ple_gate[0]), "w_pproj": f(w_ple_proj[0]),
    }
    in_maps = []
    for c in range(8):
        b, half = c // 2, c % 2
        xo = x[b, half * NTOK:(half + 1) * NTOK, :].T
        xp = x[b, 0:NTOK, :].T if half == 1 else np.zeros((D, NTOK), np.float32)
        m = dict(shared)
        m["xT"] = f(np.concatenate([xp, xo], axis=1))
        m["pT"] = f(p[0, b, half * NTOK:(half + 1) * NTOK, :].T)
        in_maps.append(m)
    if _NC[0] is None:
        _NC[0] = build_program()
    res = run_bass_kernel_spmd(_NC[0], in_maps, core_ids=list(range(8)))
    out = np.zeros((4, 4096, D), np.float32)
    for c in range(8):
        b, half = c // 2, c % 2
        out[b, half * NTOK:(half + 1) * NTOK, :] = res.results[c]["outT"].T
    return out
```

```python
import numpy as np
from contextlib import ExitStack
import concourse.bass as bass
import concourse.mybir as mybir
from concourse.bass_utils import run_bass_kernel_spmd

F32 = mybir.dt.float32
BF16 = mybir.dt.bfloat16
AF = mybir.ActivationFunctionType
ALU = mybir.AluOpType
AX = mybir.AxisListType

D = 1024
NTOK = 2048
TA = 256
NTA = 16
TB = 512
NTB = 4
C = 64
DEC = 0.6065306597126334
GN_EPS = 64e-5
RMS_EPS = 1e-6

GMIX, GMLP, GPLE, GFIN, MU, CW, W0, A0, KK, KA, RK, LG, LB, NP = 0, 8, 16, 24, 32, 47, 59, 63, 67, 71, 75, 79, 83, 87
CI, COB, CON, CML, CMT, CMI, CI8, NCC = 0, 128, 256, 384, 896, 1408, 1920, 2432


class Res:
    __slots__ = ("name", "lw", "rd")

    def __init__(self, name):
        self.name = name
        self.lw = {}
        self.rd = {}


class Sched:
    def __init__(self, nc, es):
        self.nc = nc
        self.es = es
        self.eng = {}
        hs = {"pe": nc.tensor, "dve": nc.vector, "act": nc.scalar, "pool": nc.gpsimd, "sp": nc.sync}
        for name, h in hs.items():
            sem = es.enter_context(nc.semaphore("s_" + name))
            self.eng[name] = dict(h=h, sem=sem, cnt=0, known={})
        self.ndma = 0
        self.dmatoks = []

    def _sem(self, k):
        return self.eng[k[1]]["sem"] if k[0] == "eng" else k[1]

    def op(self, e, fn, reads=(), writes=(), dma=False, inc=True, nowaw=False):
        E = self.eng[e]
        waits = {}
        me = ("eng", e)
        for r in reads:
            for k, v in r.lw.items():
                if v > waits.get(k, 0):
                    waits[k] = v
        allsame = e in ("dve", "act", "pool")
        for w in writes:
            if not nowaw:
                for k, v in w.lw.items():
                    if (k != me or allsame) and v > waits.get(k, 0):
                        waits[k] = v
            for k, v in w.rd.items():
                if (k != me or allsame) and v > waits.get(k, 0):
                    waits[k] = v
        kn = E["known"]
        for k, v in waits.items():
            if kn.get(k, 0) >= v:
                continue
            E["h"].wait_ge(self._sem(k), v)
            kn[k] = v
        inst = fn(E["h"])
        if dma:
            sem = self.es.enter_context(self.nc.semaphore("d%d" % self.ndma))
            self.ndma += 1
            inst.then_inc(sem, 16)
            tok = (("dma", sem), 16)
            self.dmatoks.append(tok)
        elif inc:
            E["cnt"] += 1
            inst.then_inc(E["sem"], 1)
            tok = (me, E["cnt"])
        else:
            tok = (me, E["cnt"] + 1)
        k, v = tok
        for r in reads:
            if v > r.rd.get(k, 0):
                r.rd[k] = v
        for w in writes:
            if v > w.lw.get(k, 0):
                w.lw[k] = v
        return inst

    def prewait(self, e, reads=(), writes=()):
        E = self.eng[e]
        waits = {}
        me = ("eng", e)
        for r in reads:
            for k, v in r.lw.items():
                if k != me and v > waits.get(k, 0):
                    waits[k] = v
        for w in writes:
            for k, v in list(w.lw.items()) + list(w.rd.items()):
                if k != me and v > waits.get(k, 0):
                    waits[k] = v
        kn = E["known"]
        for k, v in waits.items():
            if kn.get(k, 0) >= v:
                continue
            E["h"].wait_ge(self._sem(k), v)
            kn[k] = v

    def barrier(self):
        for e, E in self.eng.items():
            for o, O in self.eng.items():
                if o == e or O["cnt"] == 0:
                    continue
                k = ("eng", o)
                if E["known"].get(k, 0) < O["cnt"]:
                    E["h"].wait_ge(O["sem"], O["cnt"])
                    E["known"][k] = O["cnt"]
            for k, v in self.dmatoks:
                if E["known"].get(k, 0) < v:
                    E["h"].wait_ge(k[1], v)
                    E["known"][k] = v
        self.dmatoks = []


SAME_ENGINE_ALL = True


def build_program(debug=False):
    nc = bass.Bass("TRN2", target_bir_lowering=False)
    dt = nc.dram_tensor
    xT = dt("xT", [D, 2 * NTOK], F32, kind="ExternalInput").ap()
    pT = dt("pT", [256, NTOK], F32, kind="ExternalInput").ap()
    prm_d = dt("prm", [128, NP], F32, kind="ExternalInput").ap()
    cst_d = dt("cst", [128, NCC], F32, kind="ExternalInput").ap()
    smk_d = dt("smk", [128, 4 * TA], F32, kind="ExternalInput").ap()
    w_in_d = dt("w_in", [D, 3360], F32, kind="ExternalInput").ap()
    lora_d = dt("lora", [128, 512], F32, kind="ExternalInput").ap()
    glora_d = dt("glora", [160, 512], F32, kind="ExternalInput").ap()
    w_out_d = dt("w_out", [D, D], F32, kind="ExternalInput").ap()
    w_up_d = dt("w_up", [D, 4096], F32, kind="ExternalInput").ap()
    w_dn_d = dt("w_down", [4096, D], F32, kind="ExternalInput").ap()
    w_gt_d = dt("w_gate", [D, D], F32, kind="ExternalInput").ap()
    w_pp_d = dt("w_pproj", [256, D], F32, kind="ExternalInput").ap()
    outT = dt("outT", [D, NTOK], F32, kind="ExternalOutput").ap()
    if debug:
        dbg_yrw = dt("dbg_yrw", [128, 4 * NTOK], F32, kind="ExternalOutput").ap()
        dbg_h = dt("dbg_h", [128, 8 * NTOK], F32, kind="ExternalOutput").ap()

    with ExitStack() as es:
        S = Sched(nc, es)
        op = S.op

        def sbt(stack, name, shape, dtype):
            return stack.enter_context(nc.sbuf_tensor("sb_" + name, shape, dtype)), Res(name)

        bigt = [es.enter_context(nc.psum_tensor("pbig%d" % k, [128, 1024], F32)) for k in range(4)]
        bankR = [Res("bank%d" % j) for j in range(8)]
        singles = [(bigt[j // 2][:, (j % 2) * 512:(j % 2 + 1) * 512], bankR[j]) for j in range(8)]
        bigs = [(bigt[k], [bankR[2 * k], bankR[2 * k + 1]]) for k in range(4)]
        ring = [0, 0]
        ringset = [singles[0:5]]
        pybank = singles[7]

        def bank():
            rs = ringset[0]
            b = rs[ring[0] % len(rs)]
            ring[0] += 1
            return b

        ringc = [0]

        def cbank():
            b = singles[5 + ringc[0] % 2]
            ringc[0] += 1
            return b

        def bigbank():
            b = bigs[ring[1] % 4]
            ring[1] += 1
            return b

        def bf(pb):
            return pb[:, :].bitcast(BF16)

        prm, Rprm = sbt(es, "prm", [128, NP], F32)
        omm, Romm = sbt(es, "omm", [128, 15], F32)
        oka, Roka = sbt(es, "oka", [128, 4], F32)
        cst, Rcst = sbt(es, "cst", [128, 384], BF16)
        idf, Ridf = sbt(es, "idf", [128, 128], F32)
        yrw, Ryrw = sbt(es, "yrw", [128, 4, NTOK], BF16)
        epsb, Repsb = sbt(es, "epsb", [128, 4], F32)

        x2, Rx2 = sbt(es, "x2", [128, 8, 2], F32)
        op("sp", lambda h: h.dma_start(out=prm[:], in_=prm_d), writes=[Rprm], dma=True)
        op("sp", lambda h: h.dma_start(out=x2[:], in_=xT[:, NTOK - 2:NTOK].rearrange("(c p) t -> p c t", p=128)), writes=[Rx2], dma=True)
        op("sp", lambda h: h.dma_start(out=idf[:], in_=cst_d[:, 0:128]), writes=[Ridf], dma=True)
        op("pool", lambda h: h.dma_start(out=cst[:], in_=cst_d[:, 0:384], max_dma_last_dim=4096), writes=[Rcst], dma=True)
        op("dve", lambda h: h.tensor_scalar(out=omm[:], in0=prm[:, MU:MU + 15], scalar1=-1.0, scalar2=1.0, op0=ALU.mult, op1=ALU.add), reads=[Rprm], writes=[Romm])
        op("dve", lambda h: h.tensor_scalar(out=oka[:], in0=prm[:, KA:KA + 4], scalar1=-1.0, scalar2=1.0, op0=ALU.mult, op1=ALU.add), reads=[Rprm], writes=[Roka])
        op("dve", lambda h: h.memset(epsb[:, 0:1], RMS_EPS), writes=[Repsb])
        op("dve", lambda h: h.memset(epsb[:, 1:2], GN_EPS), writes=[Repsb])

        ident = cst[:, CI:CI + 128]
        onesblk = cst[:, COB:COB + 128]
        ones = cst[:, CON:CON + 128]

        def rmsnorm(bufs, src, dst, gcol, T, Rsrc, Rdst):
            sq, Rsq, lnv, Rlnv, rstd, Rrstd = bufs
            Rsrc_l = Rsrc if isinstance(Rsrc, list) else [Rsrc]
            for c in range(8):
                op("act", lambda h, c=c: h.activation(out=sq[:, c, 0:T], in_=src(c), func=AF.Square), reads=Rsrc_l, writes=[Rsq])
            pb, Rpb = bank()
            for c in range(8):
                op("pe", lambda h, c=c: h.matmul(pb[:, 0:T], lhsT=ones, rhs=sq[:, c, 0:T], start=(c == 0), stop=(c == 7)),
                   reads=[Rsq, Rcst], writes=[Rpb], inc=(c == 7))
            op("act", lambda h: h.activation(out=lnv[:, 0:T], in_=pb[:, 0:T], func=AF.Ln, bias=epsb[:, 0:1], scale=1.0 / D), reads=[Rpb, Repsb], writes=[Rlnv])
            op("act", lambda h: h.activation(out=rstd[:, 0:T], in_=lnv[:, 0:T], func=AF.Exp, scale=-0.5), reads=[Rlnv], writes=[Rrstd])
            for c in range(8):
                op("dve", lambda h, c=c: h.scalar_tensor_tensor(out=dst(c), in0=src(c), scalar=prm[:, gcol + c:gcol + c + 1], in1=rstd[:, 0:T], op0=ALU.mult, op1=ALU.mult),
                   reads=Rsrc_l + [Rrstd, Rprm], writes=[Rdst])

        def rms_sq(bufs, src, T, Rsrc):
            sq, Rsq, lnv, Rlnv, rstd, Rrstd = bufs
            Rsrc_l = Rsrc if isinstance(Rsrc, list) else [Rsrc]
            for c in range(8):
                op("act", lambda h, c=c: h.activation(out=sq[:, c, 0:T], in_=src(c), func=AF.Square), reads=Rsrc_l, writes=[Rsq])

        def rms_fin(bufs, src, dst, gcol, T, Rsrc, Rdst):
            sq, Rsq, lnv, Rlnv, rstd, Rrstd = bufs
            Rsrc_l = Rsrc if isinstance(Rsrc, list) else [Rsrc]
            pb, Rpb = bank()
            for c in range(8):
                op("pe", lambda h, c=c: h.matmul(pb[:, 0:T], lhsT=ones, rhs=sq[:, c, 0:T], start=(c == 0), stop=(c == 7)),
                   reads=[Rsq, Rcst], writes=[Rpb], inc=(c == 7))
            op("act", lambda h: h.activation(out=lnv[:, 0:T], in_=pb[:, 0:T], func=AF.Ln, bias=epsb[:, 0:1], scale=1.0 / D), reads=[Rpb, Repsb], writes=[Rlnv])
            op("act", lambda h: h.activation(out=rstd[:, 0:T], in_=lnv[:, 0:T], func=AF.Exp, scale=-0.5), reads=[Rlnv], writes=[Rrstd])
            for c in range(8):
                op("dve", lambda h, c=c: h.scalar_tensor_tensor(out=dst(c), in0=src(c), scalar=prm[:, gcol + c:gcol + c + 1], in1=rstd[:, 0:T], op0=ALU.mult, op1=ALU.mult),
                   reads=Rsrc_l + [Rrstd, Rprm], writes=[Rdst])

        def run(g):
            for _ in g:
                pass

        def interleave(a, b, na=1, nb=2):
            da = db = False
            while not (da and db):
                for _ in range(na):
                    if not da:
                        try:
                            next(a)
                        except StopIteration:
                            da = True
                for _ in range(nb):
                    if not db:
                        try:
                            next(b)
                        except StopIteration:
                            db = True

        wmlp, Rwconv = sbt(es, "wmlp", [128, 16384], BF16)
        wconv = wmlp[:, 0:8 * 1536].rearrange("p (c n) -> p c n", c=8)
        with ExitStack() as sa:
            WR = 1824
            w_in = wmlp[:, 0:8 * WR].rearrange("p (c n) -> p c n", c=8)
            Rw_in = Rwconv
            lora, Rlora = sbt(sa, "lora", [128, 512], BF16)
            glora, Rglora = sbt(sa, "glora", [128, 2, 512], BF16)
            msk, Rmsk = sbt(sa, "msk", [128, 2048], BF16)
            smk, Rsmk = sbt(sa, "smk", [128, 4 * TA], F32)
            for c in range(8):
                op("pool", lambda h, c=c: h.dma_start(out=w_in[:, c, :], in_=w_in_d[c * 128:(c + 1) * 128, 1536:3360], max_dma_last_dim=4096), writes=[Rw_in], dma=True, nowaw=True)
            op("pool", lambda h: h.dma_start(out=lora[:], in_=lora_d, max_dma_last_dim=4096), writes=[Rlora], dma=True)
            op("pool", lambda h: h.dma_start(out=glora[:, 0, :], in_=glora_d[0:128, :], max_dma_last_dim=4096), writes=[Rglora], dma=True)
            op("pool", lambda h: h.dma_start(out=glora[0:32, 1, :], in_=glora_d[128:160, :], max_dma_last_dim=4096), writes=[Rglora], dma=True, nowaw=True)
            op("pool", lambda h: h.dma_start(out=msk[:], in_=cst_d[:, 384:2432], max_dma_last_dim=4096), writes=[Rmsk], dma=True)
            op("sp", lambda h: h.dma_start(out=smk[:], in_=smk_d), writes=[Rsmk], dma=True)

            def mask2(k):
                return msk[:, k * 512:(k + 1) * 512].unsqueeze(1).broadcast_to([128, 2, 512])

            sq, Rsq = sbt(sa, "sq", [128, 8, TA], BF16)
            lnv, Rlnv = sbt(sa, "lnv", [128, TA], F32)
            rstd, Rrstd = sbt(sa, "rstd", [128, TA], F32)
            xn, Rxn = sbt(sa, "xn", [128, 8, TA], BF16)
            nbufs = (sq, Rsq, lnv, Rlnv, rstd, Rrstd)
            shs = [sbt(sa, "sh%d" % i, [128, TA + 1], F32) for i in range(3)]
            tmps = [sbt(sa, "tm%d" % i, [128, TA], F32) for i in range(3)]
            halo, Rhalo = sbt(sa, "halo", [128, 16], F32)
            ru, Rru = sbt(sa, "ru", [128, 4, TA], F32)
            ku, Rku = sbt(sa, "ku", [128, 4, TA], F32)
            vu, Rvu = sbt(sa, "vu", [128, 4, TA], F32)
            xwa, Rxwa = sbt(sa, "xwa", [128, TA], F32)
            xg, Rxg = sbt(sa, "xg", [128, 2, TA], F32)
            tw, Rtw = sbt(sa, "tw", [128, TA], BF16)
            sg, Rsg = sbt(sa, "sg", [128, 2, TA], BF16)
            t12, _ = sbt(sa, "t12", [128, 8, TA], F32)
            xt = t12
            t0_, _ = sbt(sa, "t0", [128, 4, TA], F32)
            t3_, _ = sbt(sa, "t3", [128, 4, TA], F32)
            t4_, _ = sbt(sa, "t4", [128, 4, TA], F32)
            T5b = [t0_[:], t12[:, 0:4, :], t12[:, 4:8, :], t3_[:], t4_[:]]
            T5r = [[Res("t%d_%d" % (k, hp)) for hp in range(2)] for k in range(5)]
            Rxt = T5r[1] + T5r[2]

            def half2(n, shape, dtype):
                t, _ = sbt(sa, n, shape, dtype)
                return t, [Res(n + "_0"), Res(n + "_1")]
            ksq, Rksq = half2("ksq", [128, 4, TA], BF16)
            rkb, Rrkb = half2("rkb", [128, 4, TA], BF16)
            egp, Regp = half2("egp", [128, 4, TA], F32)
            Atz, RAtz = half2("Atz", [128, 4, 2, TA], BF16)
            Rtz, RRtz = half2("Rtz", [128, 4, 2, TA], BF16)
            Ktc, RKtc = half2("Ktc", [128, 4, TA], BF16)
            Btc, RBtc = half2("Btc", [128, 4, TA], BF16)
            vch, Rvch = half2("vch", [128, 4, TA], BF16)
            gate2, _ = sbt(sa, "gate2", [128, 2, 4, TA], BF16)
            bonus2, _ = sbt(sa, "bonus2", [128, 2, 4, TA], BF16)
            Rgate2 = [[Res("gate%d_%d" % (a, b)) for b in range(2)] for a in range(2)]
            Rbonus2 = [[Res("bonus%d_%d" % (a, b)) for b in range(2)] for a in range(2)]
            gC, _ = sbt(sa, "gC", [128, 2, 4, 4], F32)
            RgC = [Res("gC0"), Res("gC1")]
            ych, Rych = sbt(sa, "ych", [128, 4, TA], F32)
            KBt = [sbt(sa, "KBt%d" % i, [128, 1024], BF16) for i in range(2)]
            Vt = [sbt(sa, "Vt%d" % i, [128, 512], BF16) for i in range(2)]
            Lb = [[sbt(sa, "Lb%d_%d" % (k, i), [128, 1024], BF16) for i in range(2)] for k in range(2)]
            Ltb = [[sbt(sa, "Ltb%d_%d" % (k, i), [128, 1024], BF16) for i in range(2)] for k in range(2)]
            Sb = [[sbt(sa, "Sb%d_%d" % (k, i), [128, 1024], BF16) for i in range(2)] for k in range(2)]
            AakTs = [sbt(sa, "AakT%d" % i, [128, 1024], BF16) for i in range(2)]
            ArbT = [sbt(sa, "ArbT%d" % i, [128, 1024], BF16) for i in range(2)]
            ArkT = [sbt(sa, "ArkT%d" % i, [128, 1024], BF16) for i in range(2)]
            Zsb = [sbt(sa, "Zsb%d" % i, [128, 512], F32) for i in range(2)]
            Xs, RXs = sbt(sa, "Xs", [128, 512], BF16)
            Ub, RUb = sbt(sa, "Ub", [128, 512], BF16)
            Hf, RHf = sbt(sa, "Hf", [128, 4, 128], F32)
            Hb, RHb = sbt(sa, "Hb", [128, 4, 128], BF16)
            ytok, Rytok = sbt(sa, "ytok", [128, 512], F32)
            ysq, Rysq = sbt(sa, "ysq", [128, 512], F32)
            ycn, Rycn = sbt(sa, "ycn", [128, 512], F32)
            st, Rst = sbt(sa, "st", [128, 6, 8], F32)

            op("pool", lambda h: h.memset(halo[:], 0.0), writes=[Rhalo])
            op("pool", lambda h: h.memset(Hf[:], 0.0), writes=[RHf])
            op("pool", lambda h: h.memset(Hb[:], 0.0), writes=[RHb])
            op("pool", lambda h: h.memset(Atz[:], 0.0), writes=RAtz)
            op("pool", lambda h: h.memset(Rtz[:], 0.0), writes=RRtz)
            op("pool", lambda h: h.memset(xg[:], 0.0), writes=[Rxg])
            op("pool", lambda h: h.memset(sg[:], 0.0), writes=[Rsg])
            op("pool", lambda h: h.memset(ru[:], 0.0), writes=[Rru])

            shc = [0]

            def proj_shift_multi(specs):
                pbs = [bank() for _ in specs]
                S.prewait("pe", reads=[Rw_in, Rxn], writes=[r for _, r in pbs])
                n = len(specs)
                for si, (lo, M, j, dst, Rdst) in enumerate(specs):
                    pb, Rpb = pbs[si]
                    for c in range(8):
                        op("pe", lambda h, c=c: h.matmul(pb[0:M, 0:TA], lhsT=w_in[:, c, lo:lo + M], rhs=xn[:, c, :], start=(c == 0), stop=(c == 7)),
                           reads=[Rw_in, Rxn], writes=[Rpb], inc=(c == 7 and si == n - 1))
                for si, (lo, M, j, dst, Rdst) in enumerate(specs):
                    pb, Rpb = pbs[si]
                    sh, Rsh = shs[shc[0] % 3]
                    tm, Rtm = tmps[shc[0] % 3]
                    shc[0] += 1
                    op("act", lambda h: h.activation(out=sh[0:M, 1:TA + 1], in_=pb[0:M, 0:TA], func=AF.Copy), reads=[Rpb], writes=[Rsh])
                    op("pool", lambda h: h.tensor_copy(out=sh[0:M, 0:1], in_=halo[0:M, j:j + 1]), reads=[Rhalo], writes=[Rsh])
                    op("pool", lambda h: h.tensor_scalar(out=tm[0:M, :], in0=sh[0:M, 0:TA], scalar1=prm[0:M, MU + j:MU + j + 1], scalar2=0.0, op0=ALU.mult, op1=ALU.add),
                       reads=[Rsh, Rprm], writes=[Rtm])
                    op("dve", lambda h: h.scalar_tensor_tensor(out=dst, in0=sh[0:M, 1:TA + 1], scalar=omm[0:M, j:j + 1], in1=tm[0:M, :], op0=ALU.mult, op1=ALU.add),
                       reads=[Rsh, Rtm, Romm], writes=[Rdst])
                    op("pool", lambda h: h.tensor_copy(out=halo[0:M, j:j + 1], in_=sh[0:M, TA:TA + 1]), reads=[Rsh], writes=[Rhalo])

            def gen_NP(i):
                full = i >= 7
                col0 = i * TA
                op("sp", lambda h: h.dma_start(out=xt[:], in_=xT[:, col0:col0 + TA].rearrange("(c p) t -> p c t", p=128)), writes=Rxt, dma=True)
                rmsnorm(nbufs, lambda c: xt[:, c, :], lambda c: xn[:, c, :], GMIX, TA, Rxt, Rxn)
                yield
                specs = [(1536, 128, 12, xwa[:], Rxwa)]
                if full:
                    specs.append((1664, 128, 13, xg[:, 0, :], Rxg))
                    specs.append((1792, 32, 14, xg[0:32, 1, :], Rxg))
                for p in range(4):
                    if full:
                        specs.append((p * 128, 128, p, ru[:, p, :], Rru))
                    specs.append((512 + p * 128, 128, 4 + p, ku[:, p, :], Rku))
                    specs.append((1024 + p * 128, 128, 8 + p, vu[:, p, :], Rvu))
                for k in range(0, len(specs), 3):
                    proj_shift_multi(specs[k:k + 3])
                    yield

            def flat(t):
                return t[:].rearrange("p a b -> p (a b)")

            def emit_Epre(i):
                own = i >= 8
                op("act", lambda h: h.activation(out=tw[0:64, :], in_=xwa[0:64, :], func=AF.Tanh), reads=[Rxwa], writes=[Rtw])
                op("act", lambda h: h.activation(out=tw[64:128, :], in_=xwa[64:128, :], func=AF.Copy), reads=[Rxwa], writes=[Rtw])
                if own:
                    op("act", lambda h: h.activation(out=sg[:, 0, :], in_=xg[:, 0, :], func=AF.Sigmoid), reads=[Rxg], writes=[Rsg])
                    op("act", lambda h: h.activation(out=sg[0:32, 1, :], in_=xg[0:32, 1, :], func=AF.Sigmoid), reads=[Rxg], writes=[Rsg])

            def gen_E(i, hp):
                own = i >= 8
                ps_ = slice(2 * hp, 2 * hp + 2)

                def hv(t):
                    return t[:, ps_, :].rearrange("p a b -> p (a b)")
                t0, t1, t2, t3, t4 = T5b
                Rt0, Rt1, Rt2, Rt3, Rt4 = [T5r[k][hp] for k in range(5)]
                par = i % 2
                gate = gate2[:, par]
                bonus = bonus2[:, par]
                Rk, Rr, Re, RA, RR, RKt_, RBt_, Rv = [x[hp] for x in (Rksq, Rrkb, Regp, RAtz, RRtz, RKtc, RBtc, Rvch)]
                Rg = Rgate2[par][hp]
                Rbo = Rbonus2[par][hp]
                lb = {}
                bw = bank()
                ba = bank()
                bg = bank() if own else None
                for qi, p in enumerate((2 * hp, 2 * hp + 1)):
                    lb[("w", p)] = (bw[0][:, qi * TA:(qi + 1) * TA], bw[1])
                    lb[("a", p)] = (ba[0][:, qi * TA:(qi + 1) * TA], ba[1])
                    if own:
                        lb[("g", p)] = (bg[0][:, qi * TA:(qi + 1) * TA], bg[1])
                S.prewait("pe", reads=[Rlora, Rtw, Rglora, Rsg], writes=[r for _, r in lb.values()])
                for p in (2 * hp, 2 * hp + 1):
                    pw, Rpw = lb[("w", p)]
                    pa_, Rpa_ = lb[("a", p)]
                    last = (p == 2 * hp + 1)
                    op("pe", lambda h: h.matmul(pw[:, 0:TA], lhsT=lora[0:64, p * 128:(p + 1) * 128], rhs=tw[0:64, :], start=True, stop=True), reads=[Rlora, Rtw], writes=[Rpw], inc=False)
                    op("pe", lambda h: h.matmul(pa_[:, 0:TA], lhsT=lora[64:128, p * 128:(p + 1) * 128], rhs=tw[64:128, :], start=True, stop=True), reads=[Rlora, Rtw], writes=[Rpa_], inc=(last and not own))
                    if own:
                        pg, Rpg = lb[("g", p)]
                        op("pe", lambda h: h.matmul(pg[:, 0:TA], lhsT=glora[:, 0, p * 128:(p + 1) * 128], rhs=sg[:, 0, :], start=True, stop=False), reads=[Rglora, Rsg], writes=[Rpg], inc=False)
                        op("pe", lambda h: h.matmul(pg[:, 0:TA], lhsT=glora[0:32, 1, p * 128:(p + 1) * 128], rhs=sg[0:32, 1, :], start=False, stop=True), reads=[Rglora, Rsg], writes=[Rpg], inc=last)
                for p in (2 * hp, 2 * hp + 1):
                    pw, Rpw = lb[("w", p)]
                    pa_, Rpa_ = lb[("a", p)]
                    op("act", lambda h: h.activation(out=t0[:, p, :], in_=pw[:, 0:TA], func=AF.Sigmoid, bias=prm[:, W0 + p:W0 + p + 1]), reads=[Rpw, Rprm], writes=[Rt0])
                    op("act", lambda h: h.activation(out=t1[:, p, :], in_=pa_[:, 0:TA], func=AF.Sigmoid, bias=prm[:, A0 + p:A0 + p + 1]), reads=[Rpa_, Rprm], writes=[Rt1])
                    if own:
                        pg, Rpg = lb[("g", p)]
                        op("act", lambda h: h.activation(out=gate[:, p, :], in_=pg[:, 0:TA], func=AF.Copy), reads=[Rpg], writes=[Rg])
                yield
                op("dve", lambda h: h.tensor_tensor_scan(out=hv(t2), data0=smk[:, 0:2 * TA], data1=hv(t0), initial=0.0, op0=ALU.mult, op1=ALU.add), reads=[Rsmk, Rt0], writes=[Rt2])
                yield
                op("pool", lambda h: h.tensor_tensor(out=hv(t3), in0=hv(t2), in1=hv(t0), op=ALU.subtract), reads=[Rt2, Rt0], writes=[Rt3])
                op("act", lambda h: h.activation(out=hv(egp), in_=hv(t2), func=AF.Exp, scale=-DEC), reads=[Rt2], writes=[Re])
                op("act", lambda h: h.activation(out=hv(t0), in_=hv(t2), func=AF.Exp, scale=DEC), reads=[Rt2], writes=[Rt0])
                yield
                op("act", lambda h: h.activation(out=hv(t3), in_=hv(t3), func=AF.Exp, scale=-DEC), reads=[Rt3], writes=[Rt3])
                op("act", lambda h: h.activation(out=hv(vch), in_=hv(vu), func=AF.Copy), reads=[Rvu], writes=[Rv])
                for p in (2 * hp, 2 * hp + 1):
                    op("pool", lambda h: h.tensor_scalar(out=t2[:, p, :], in0=ku[:, p, :], scalar1=prm[:, KK + p:KK + p + 1], scalar2=0.0, op0=ALU.mult, op1=ALU.add), reads=[Rku, Rprm], writes=[Rt2])
                    op("act", lambda h: h.activation(out=ksq[:, p, :], in_=ku[:, p, :], func=AF.Square, scale=prm[:, KK + p:KK + p + 1]), reads=[Rku, Rprm], writes=[Rk])
                yield
                pn, Rpn = bank()
                for q in range(2):
                    p = hp * 2 + q
                    op("pe", lambda h: h.matmul(pn[:, q * TA:(q + 1) * TA], lhsT=onesblk, rhs=ksq[:, p, :], start=True, stop=True), reads=[Rcst, Rk], writes=[Rpn], inc=(q == 1))
                op("dve", lambda h: h.tensor_scalar_max(out=hv(t4), in0=pn[:, :], scalar1=1e-18), reads=[Rpn], writes=[Rt4])
                yield
                op("act", lambda h: h.activation(out=hv(t4), in_=hv(t4), func=AF.Ln), reads=[Rt4], writes=[Rt4])
                op("act", lambda h: h.activation(out=hv(t4), in_=hv(t4), func=AF.Exp, scale=-0.5), reads=[Rt4], writes=[Rt4])
                yield
                op("dve", lambda h: h.tensor_tensor(out=hv(t2), in0=hv(t2), in1=hv(t4), op=ALU.mult), reads=[Rt2, Rt4], writes=[Rt2])
                for p in (2 * hp, 2 * hp + 1):
                    op("pool", lambda h: h.tensor_scalar(out=t4[:, p, :], in0=t1[:, p, :], scalar1=prm[:, KA + p:KA + p + 1], scalar2=oka[:, p:p + 1], op0=ALU.mult, op1=ALU.add), reads=[Rt1, Rprm, Roka], writes=[Rt4])
                yield
                op("dve", lambda h: h.tensor_tensor(out=hv(t4), in0=hv(ku), in1=hv(t4), op=ALU.mult), reads=[Rku, Rt4], writes=[Rt4])
                op("pool", lambda h: h.tensor_tensor(out=hv(t1), in0=hv(t2), in1=hv(t1), op=ALU.mult), reads=[Rt2, Rt1], writes=[Rt1])
                yield
                op("dve", lambda h: h.tensor_tensor(out=hv(Ktc), in0=hv(t4), in1=hv(t0), op=ALU.mult), reads=[Rt4, Rt0], writes=[RKt_])
                op("dve", lambda h: h.tensor_tensor(out=hv(Btc), in0=hv(t1), in1=hv(t0), op=ALU.mult), reads=[Rt1, Rt0], writes=[RBt_])
                yield
                if own:
                    for p in (2 * hp, 2 * hp + 1):
                        op("dve", lambda h: h.scalar_tensor_tensor(out=rkb[:, p, :], in0=ru[:, p, :], scalar=prm[:, RK + p:RK + p + 1], in1=t4[:, p, :], op0=ALU.mult, op1=ALU.mult), reads=[Rru, Rprm, Rt4], writes=[Rr])
                    pbn, Rpbn = bank()
                    for q in range(2):
                        p = hp * 2 + q
                        op("pe", lambda h: h.matmul(pbn[:, q * TA:(q + 1) * TA], lhsT=onesblk, rhs=rkb[:, p, :], start=True, stop=True), reads=[Rcst, Rr], writes=[Rpbn], inc=(q == 1))
                    op("dve", lambda h: h.tensor_tensor(out=hv(bonus), in0=hv(vu), in1=pbn[:, :], op=ALU.mult), reads=[Rvu, Rpbn], writes=[Rbo])
                    yield

            def emit_Efin(i):
                own = i >= 8
                par = i % 2
                t2, t3 = T5b[2], T5b[3]
                for hp in range(2):
                    ps_ = slice(2 * hp, 2 * hp + 2)
                    for e in range(2):
                        sl = slice(e * 64, (e + 1) * 64)
                        op("dve", lambda h: h.scalar_tensor_tensor(out=Atz[sl, ps_, e, :], in0=t2[sl, ps_, :], scalar=-1.0, in1=t3[sl, ps_, :], op0=ALU.mult, op1=ALU.mult),
                           reads=[T5r[2][hp], T5r[3][hp]], writes=[RAtz[hp]])
                        if own:
                            op("pool", lambda h: h.tensor_tensor(out=Rtz[sl, ps_, e, :], in0=ru[sl, ps_, :], in1=egp[sl, ps_, :], op=ALU.mult), reads=[Rru, Regp[hp]], writes=[RRtz[hp]])
                for c in range(4):
                    op("pool", lambda h: h.tensor_copy(out=gC[:, par, :, c:c + 1], in_=egp[:, :, c * 64 + 63:c * 64 + 64]), reads=Regp, writes=[RgC[par]])

            def gen_E2(i):
                emit_Epre(i)
                a_, b_ = gen_E(i, 0), gen_E(i, 1)
                da = db = False
                while not (da and db):
                    if not da:
                        try:
                            next(a_)
                        except StopIteration:
                            da = True
                    if not db:
                        try:
                            next(b_)
                        except StopIteration:
                            db = True
                    yield

            def gen_front(i):
                yield from gen_NP(i)
                yield from gen_E2(i)

            def gen_prep2(i):
                own = i >= 8
                BS = [slice(blk * 128, (blk + 1) * 128) for blk in range(2)]

                def v2(t):
                    return t[:].rearrange("p (a b) -> p a b", a=2)

                pk = [bank(), bank()]
                pv, Rpv = bank()
                S.prewait("pe", reads=RKtc + RBtc + Rvch + [Rcst], writes=[pk[0][1], pk[1][1], Rpv])
                pvv = bf(pv)
                for blk in range(2):
                    pkbv = bf(pk[blk][0])
                    for p in range(4):
                        op("pe", lambda h, p=p: h.transpose(out=pkbv[:, p * 128:(p + 1) * 128], in_=Ktc[:, p, BS[blk]], identity=ident), reads=RKtc + [Rcst], writes=[pk[blk][1]], inc=False)
                    for p in range(4):
                        op("pe", lambda h, p=p: h.transpose(out=pkbv[:, 512 + p * 128:512 + (p + 1) * 128], in_=Btc[:, p, BS[blk]], identity=ident), reads=RBtc + [Rcst], writes=[pk[blk][1]], inc=False)
                for blk in range(2):
                    for p in range(4):
                        op("pe", lambda h, p=p: h.transpose(out=pvv[:, blk * 512 + p * 128:blk * 512 + (p + 1) * 128], in_=vch[:, p, BS[blk]], identity=ident), reads=Rvch + [Rcst], writes=[Rpv], inc=(blk == 1 and p == 3))
                for blk in range(2):
                    kbt, Rkbt = KBt[blk]
                    op("act", lambda h: h.activation(out=kbt[:], in_=bf(pk[blk][0]), func=AF.Copy), reads=[pk[blk][1]], writes=[Rkbt])
                    vt, Rvt = Vt[blk]
                    op("dve", lambda h: h.tensor_copy(out=vt[:], in_=pvv[:, blk * 512:(blk + 1) * 512]), reads=[Rpv], writes=[Rvt])
                yield

                def amat2(lhs, Rl, rhs, Rr, lz, rz, outs, mk):
                    pbs = [bigbank(), bigbank()]
                    S.prewait("pe", reads=Rl + Rr, writes=pbs[0][1] + pbs[1][1])
                    for blk in range(2):
                        pb, Rpb = pbs[blk]
                        for hh in range(8):
                            p, e = hh // 2, hh % 2
                            l_ap = lhs[:, p, e, BS[blk]] if lz else lhs[:, p, BS[blk]]
                            r_ap = rhs[:, p, e, BS[blk]] if rz else rhs[:, p, BS[blk]]
                            op("pe", lambda h: h.matmul(pb[:, hh * 128:(hh + 1) * 128], lhsT=l_ap, rhs=r_ap, start=True, stop=True),
                               reads=Rl + Rr, writes=Rpb, inc=(blk == 1 and hh == 7))
                    for blk in range(2):
                        pb, Rpb = pbs[blk]
                        o, Ro = outs[blk]
                        op("dve", lambda h: h.tensor_tensor(out=v2(o), in0=v2(pb), in1=mask2(mk), op=ALU.mult), reads=Rpb + [Rmsk], writes=[Ro])

                amat2(Atz, RAtz, Btc, RBtc, True, False, [Lb[0][0], Lb[1][0]], 0)
                yield
                amat2(Btc, RBtc, Atz, RAtz, False, True, [Ltb[0][0], Ltb[1][0]], 1)
                for blk in range(2):
                    S0, RS0 = Sb[blk][0]
                    Lt0, RLt0 = Ltb[blk][0]
                    op("pool", lambda h: h.tensor_tensor(out=v2(S0), in0=v2(Lt0), in1=mask2(3), op=ALU.add), reads=[RLt0, Rmsk], writes=[RS0])
                yield
                amat2(Ktc, RKtc, Atz, RAtz, False, True, AakTs, 1)
                yield
                if own:
                    amat2(Btc, RBtc, Rtz, RRtz, False, True, ArbT, 2)
                    yield
                    amat2(Ktc, RKtc, Rtz, RRtz, False, True, ArkT, 2)
                    yield

                def hmm_burst(jobs):
                    pbs = [bigbank() for _ in jobs]
                    rr = []
                    ww = []
                    for (lhs, Rl, rhs, Rr), (pb, Rpb) in zip(jobs, pbs):
                        rr += [Rl, Rr]
                        ww += Rpb
                    S.prewait("pe", reads=rr, writes=ww)
                    for ji, ((lhs, Rl, rhs, Rr), (pb, Rpb)) in enumerate(zip(jobs, pbs)):
                        for hh in range(8):
                            cs = slice(hh * 128, (hh + 1) * 128)
                            op("pe", lambda h: h.matmul(pb[:, cs], lhsT=lhs[:, cs], rhs=rhs[:, cs], start=True, stop=True), reads=[Rl, Rr], writes=Rpb,
                               inc=(ji == len(jobs) - 1 and hh == 7))
                    return pbs

                cur = 0
                for lvl in range(5):
                    for blk in range(2):
                        Lc, RLc = Lb[blk][cur]
                        Ltc, RLtc = Ltb[blk][cur]
                        Ln_, RLn_ = Lb[blk][1 - cur]
                        Ltn, RLtn = Ltb[blk][1 - cur]
                        jobs = [(Ltc, RLtc, Lc, RLc)]
                        if lvl < 4:
                            jobs.append((Lc, RLc, Ltc, RLtc))
                        pbs = hmm_burst(jobs)
                        pb, Rpb = pbs[0]
                        op("act", lambda h: h.activation(out=Ln_[:], in_=pb[:, :], func=AF.Copy), reads=Rpb, writes=[RLn_])
                        if lvl < 4:
                            pb2, Rpb2 = pbs[1]
                            if blk == 0 or lvl % 2 == 0:
                                op("dve", lambda h: h.tensor_copy(out=Ltn[:], in_=pb2[:, :]), reads=Rpb2, writes=[RLtn])
                            else:
                                op("act", lambda h: h.activation(out=Ltn[:], in_=pb2[:, :], func=AF.Copy), reads=Rpb2, writes=[RLtn])
                        yield
                    for blk in range(2):
                        Ln_, RLn_ = Lb[blk][1 - cur]
                        Sc, RSc = Sb[blk][cur]
                        Sn, RSn = Sb[blk][1 - cur]
                        pbs = hmm_burst([(Ln_, RLn_, Sc, RSc)])
                        pb3, Rpb3 = pbs[0]
                        op("dve", lambda h: h.tensor_tensor(out=Sn[:], in0=Sc[:], in1=pb3[:, :], op=ALU.add), reads=[RSc] + Rpb3, writes=[RSn])
                        yield
                    cur = 1 - cur
                pz = [bank(), bank()]
                S.prewait("pe", reads=[AakTs[0][1], AakTs[1][1], Vt[0][1], Vt[1][1]], writes=[pz[0][1], pz[1][1]])
                for blk in range(2):
                    pb, Rpb = pz[blk]
                    AakT, RAakT = AakTs[blk]
                    vt, Rvt = Vt[blk]
                    for hh in range(8):
                        op("pe", lambda h: h.matmul(pb[:, hh * 64:(hh + 1) * 64], lhsT=AakT[:, hh * 128:(hh + 1) * 128], rhs=vt[:, hh * 64:(hh + 1) * 64], start=True, stop=True),
                           reads=[RAakT, Rvt], writes=[Rpb], inc=(blk == 1 and hh == 7))
                for blk in range(2):
                    pb, Rpb = pz[blk]
                    zs, Rzs = Zsb[blk]
                    op("act", lambda h: h.activation(out=zs[:], in_=pb[:, :], func=AF.Copy), reads=[Rpb], writes=[Rzs])
                yield

            def gen_chain(i, blk):
                own = i >= 8
                kbt, Rkbt = KBt[blk]
                vt, Rvt = Vt[blk]
                arb, Rarb = ArbT[blk]
                ark, Rark = ArkT[blk]
                zs, Rzs = Zsb[blk]
                TT, RTT = Sb[blk][1]
                if own:
                    py, Rpy = pybank
                for hf in range(2):
                    rows = slice(hf * 64, (hf + 1) * 64)
                    ts = slice(blk * 128 + hf * 64, blk * 128 + (hf + 1) * 64)
                    px, Rpx = cbank()
                    for hh in range(8):
                        p, e = hh // 2, hh % 2
                        op("pe", lambda h, hh=hh, p=p, e=e: h.matmul(px[rows, hh * 64:(hh + 1) * 64], lhsT=Atz[:, p, e, ts], rhs=Hb[:, p, e * 64:(e + 1) * 64], start=True, stop=True),
                           reads=RAtz + [RHb], writes=[Rpx], inc=(hh == 7))
                    op("dve", lambda h, px=px: h.tensor_tensor(out=Xs[rows, :], in0=zs[rows, :], in1=px[rows, :], op=ALU.add), reads=[Rzs, Rpx], writes=[RXs])
                    yield
                    pu, Rpu = cbank()
                    for hh in range(8):
                        op("pe", lambda h, hh=hh: h.matmul(pu[rows, hh * 64:(hh + 1) * 64], lhsT=TT[rows, hh * 128 + hf * 64:hh * 128 + (hf + 1) * 64], rhs=Xs[rows, hh * 64:(hh + 1) * 64], start=True, stop=True),
                           reads=[RTT, RXs], writes=[Rpu], inc=(hh == 7))
                    op("act", lambda h, pu=pu: h.activation(out=Ub[rows, :], in_=pu[rows, :], func=AF.Copy), reads=[Rpu], writes=[RUb])
                    yield
                    ph, Rph = cbank()
                    S.prewait("pe", reads=RRtz + [RHb, Rarb, Rark, RUb, Rvt, Rkbt], writes=[Rph] + ([Rpy] if own else []))
                    for p in range(4):
                        cs = slice(p * 128, (p + 1) * 128)
                        op("pe", lambda h, cs=cs: h.matmul(ph[:, cs], lhsT=kbt[rows, cs], rhs=vt[rows, cs], start=True, stop=False), reads=[Rkbt, Rvt], writes=[Rph], inc=False)
                        op("pe", lambda h, cs=cs, p=p: h.matmul(ph[:, cs], lhsT=kbt[rows, 512 + p * 128:512 + (p + 1) * 128], rhs=Ub[rows, cs], start=False, stop=True), reads=[Rkbt, RUb], writes=[Rph], inc=(p == 3))
                    if own:
                        for hh in range(8):
                            p, e = hh // 2, hh % 2
                            cs = slice(hh * 64, (hh + 1) * 64)
                            tcs = slice(hh * 128 + hf * 64, hh * 128 + (hf + 1) * 64)
                            op("pe", lambda h, p=p, e=e, cs=cs: h.matmul(py[rows, cs], lhsT=Rtz[:, p, e, ts], rhs=Hb[:, p, e * 64:(e + 1) * 64], start=True, stop=False),
                               reads=RRtz + [RHb], writes=[Rpy], inc=False)
                            op("pe", lambda h, cs=cs, tcs=tcs: h.matmul(py[rows, cs], lhsT=arb[rows, tcs], rhs=Ub[rows, cs], start=False, stop=False), reads=[Rarb, RUb], writes=[Rpy], inc=False)
                            op("pe", lambda h, cs=cs, tcs=tcs: h.matmul(py[rows, cs], lhsT=ark[rows, tcs], rhs=vt[rows, cs], start=False, stop=True), reads=[Rark, Rvt], writes=[Rpy], inc=(hh == 7))
                    op("dve", lambda h, ph=ph: h.tensor_tensor(out=flat(Hf), in0=flat(Hf), in1=ph[:, :], op=ALU.add), reads=[RHf, Rph], writes=[RHf])
                    cc = blk * 2 + hf
                    gcb = gC[:, i % 2, :, cc:cc + 1].broadcast_to([128, 4, 128])
                    op("pool", lambda h, gcb=gcb: h.tensor_tensor(out=Hf[:], in0=Hf[:], in1=gcb, op=ALU.mult), reads=[RHf, RgC[i % 2]], writes=[RHf])
                    op("act", lambda h: h.activation(out=Hb[:], in_=Hf[:], func=AF.Copy), reads=[RHf], writes=[RHb])
                    yield
                if own:
                    bs = slice(blk * 128, (blk + 1) * 128)
                    op("act", lambda h: h.activation(out=ytok[:], in_=py[:, :], func=AF.Copy), reads=[Rpy], writes=[Rytok])
                    y3 = ytok[:].rearrange("p (a b) -> p a b", a=8)
                    op("dve", lambda h: h.tensor_reduce(out=st[:, 0, :], in_=y3, axis=AX.X, op=ALU.add), reads=[Rytok], writes=[Rst])
                    op("act", lambda h: h.activation(out=ysq[:], in_=ytok[:], func=AF.Square), reads=[Rytok], writes=[Rysq])
                    op("dve", lambda h: h.tensor_reduce(out=st[:, 1, :], in_=ysq[:].rearrange("p (a b) -> p a b", a=8), axis=AX.X, op=ALU.add), reads=[Rysq], writes=[Rst])
                    op("dve", lambda h: h.tensor_scalar(out=st[:, 2, :], in0=st[:, 0, :], scalar1=1.0 / 64, scalar2=None, op0=ALU.mult), reads=[Rst], writes=[Rst])
                    op("dve", lambda h: h.tensor_tensor(out=st[:, 3, :], in0=st[:, 2, :], in1=st[:, 2, :], op=ALU.mult), reads=[Rst], writes=[Rst])
                    op("dve", lambda h: h.scalar_tensor_tensor(out=st[:, 4, :], in0=st[:, 1, :], scalar=1.0 / 64, in1=st[:, 3, :], op0=ALU.mult, op1=ALU.subtract), reads=[Rst], writes=[Rst])
                    op("act", lambda h: h.activation(out=st[:, 5, :], in_=st[:, 4, :], func=AF.Ln, bias=epsb[:, 1:2]), reads=[Rst, Repsb], writes=[Rst])
                    op("act", lambda h: h.activation(out=st[:, 5, :], in_=st[:, 5, :], func=AF.Exp, scale=-0.5), reads=[Rst], writes=[Rst])
                    yield
                    mb = st[:, 2, :].unsqueeze(2).broadcast_to([128, 8, 64])
                    rb = st[:, 5, :].unsqueeze(2).broadcast_to([128, 8, 64])
                    op("dve", lambda h: h.tensor_tensor(out=ycn[:].rearrange("p (a b) -> p a b", a=8), in0=y3, in1=mb, op=ALU.subtract), reads=[Rytok, Rst], writes=[Rycn])
                    op("dve", lambda h: h.tensor_tensor(out=ysq[:].rearrange("p (a b) -> p a b", a=8), in0=ycn[:].rearrange("p (a b) -> p a b", a=8), in1=rb, op=ALU.mult), reads=[Rycn, Rst], writes=[Rysq])
                    pyt, Rpyt = bank()
                    for p in range(4):
                        op("pe", lambda h, p=p: h.transpose(out=pyt[:, p * 128:(p + 1) * 128], in_=ysq[:, p * 128:(p + 1) * 128], identity=idf[:]), reads=[Rysq, Ridf], writes=[Rpyt], inc=(p == 3))
                    op("act", lambda h: h.activation(out=ych[:, :, bs], in_=pyt[:, :].rearrange("p (a b) -> p a b", a=4), func=AF.Copy), reads=[Rpyt], writes=[Rych])
                    yield

            def emit_F(i):
                oc0 = (i - 8) * TA
                for p in range(4):
                    tm, Rtm = tmps[p % 3]
                    op("pool", lambda h, p=p, tm=tm: h.tensor_scalar(out=tm[:], in0=ych[:, p, :], scalar1=prm[:, LG + p:LG + p + 1], scalar2=prm[:, LB + p:LB + p + 1], op0=ALU.mult, op1=ALU.add),
                       reads=[Rych, Rprm], writes=[Rtm])
                    op("pool", lambda h, p=p, tm=tm: h.tensor_tensor(out=tm[:], in0=tm[:], in1=bonus2[:, i % 2, p, :], op=ALU.add), reads=[Rtm] + Rbonus2[i % 2], writes=[Rtm])
                    op("dve", lambda h, p=p, tm=tm: h.tensor_tensor(out=yrw[:, p, oc0:oc0 + TA], in0=tm[:], in1=gate2[:, i % 2, p, :], op=ALU.mult), reads=[Rtm] + Rgate2[i % 2], writes=[Ryrw])

            def empty():
                return
                yield

            run(gen_front(0))
            for i in range(NTA):
                emit_Efin(i)
                if i == NTA - 1:
                    for c in range(8):
                        op("pool", lambda h, c=c: h.dma_start(out=wconv[:, c, :], in_=w_in_d[c * 128:(c + 1) * 128, 0:1536], max_dma_last_dim=4096), writes=[Rwconv], dma=True, nowaw=True)
                run(gen_prep2(i))
                nxt = gen_front(i + 1) if i + 1 < NTA else empty()

                def both(i=i):
                    yield from gen_chain(i, 0)
                    yield from gen_chain(i, 1)
                interleave(both(), nxt, 1, 1)
                if i >= 8:
                    emit_F(i)
            S.barrier()
            if debug:
                op("pool", lambda h: h.dma_start(out=dbg_yrw, in_=yrw[:].rearrange("p a b -> p (a b)"), max_dma_last_dim=4096), reads=[Ryrw], dma=True)

        ringset[0] = singles
        with ExitStack() as sb_:
            hres, _ = sbt(sb_, "hres", [128, 8, NTOK], F32)
            Rht = [Res("hres%d" % t) for t in range(NTB)]
            big32, Rxnm = sbt(sb_, "big32", [128, 8 * NTOK], BF16)
            xnm = big32[:, :].rearrange("p (c t) -> p c t", c=8)
            obuf = big32[:, 0:8 * TB * 2].bitcast(F32).rearrange("p (c t) -> p c t", c=8)
            Robuf = Rxnm
            sq, Rsq = sbt(sb_, "sq2", [128, 8, TB], BF16)
            lnv, Rlnv = sbt(sb_, "lnv2", [128, TB], F32)
            rstd, Rrstd = sbt(sb_, "rstd2", [128, TB], F32)
            nbufs = (sq, Rsq, lnv, Rlnv, rstd, Rrstd)
            wu = [(wmlp[:, i * 4096:(i + 1) * 4096].rearrange("p (c n) -> p c n", c=8), Res("wu%d" % i)) for i in range(2)]
            wd = [(wmlp[:, 8192 + i * 4096:8192 + (i + 1) * 4096].rearrange("p (c n) -> p c n", c=4), Res("wd%d" % i)) for i in range(2)]
            hid, Rhid = sbt(sb_, "hid", [128, 8, TB], BF16)
            wsq = sbt(sb_, "wsq", [128, 8, 1024], BF16)
            wo, Rwo = wsq
            op("pool", lambda h: h.dma_start(out=wo[:], in_=w_out_d.rearrange("(c p) n -> p c n", p=128), max_dma_last_dim=4096), writes=[Rwo], dma=True)
            for t in range(NTB):
                for hc in range(2):
                    op("sp", lambda h, t=t, hc=hc: h.dma_start(out=hres[:, hc * 4:(hc + 1) * 4, t * TB:(t + 1) * TB], in_=xT[hc * 512:(hc + 1) * 512, NTOK + t * TB:NTOK + (t + 1) * TB].rearrange("(c p) t -> p c t", p=128)),
                       writes=[Rht[t]], dma=True, nowaw=True)

            with ExitStack() as s2:
                Cs = [sbt(s2, "Csb%d" % i, [128, TB], F32) for i in range(2)]
                Bs = [sbt(s2, "Bsb%d" % i, [128, TB], F32) for i in range(2)]
                ubuf, Rubuf = sbt(s2, "ubuf", [128, 4, TB + 2], BF16)
                dg, Rdg = sbt(s2, "dg", [128, 12, 128], BF16)
                ycv, Rycv = sbt(s2, "ycv", [128, 4, TB], BF16)
                for j in range(3):
                    for q in range(4):
                        op("dve", lambda h, j=j, q=q: h.tensor_scalar(out=dg[:, j * 4 + q, :], in0=ident, scalar1=prm[:, CW + j * 4 + q:CW + j * 4 + q + 1], scalar2=None, op0=ALU.mult),
                           reads=[Rcst, Rprm], writes=[Rdg])

                def cproj(lo, T):
                    pb, Rpb = bank()
                    for c in range(8):
                        op("pe", lambda h, c=c: h.matmul(pb[:, 0:T], lhsT=wconv[:, c, lo:lo + 128], rhs=hid[:, c, 0:T], start=(c == 0), stop=(c == 7)),
                           reads=[Rwconv, Rhid], writes=[Rpb], inc=(c == 7))
                    return pb, Rpb

                rmsnorm(nbufs, lambda c: x2[:, c, :], lambda c: hid[:, c, 0:2], GMIX, 2, Rx2, Rhid)
                for q in range(4):
                    Csb, RCsb = Cs[q % 2]
                    pb, Rpb = cproj(512 + q * 128, 2)
                    op("act", lambda h, pb=pb, Csb=Csb: h.activation(out=Csb[:, 0:2], in_=pb[:, 0:2], func=AF.Copy), reads=[Rpb], writes=[RCsb])
                    pb2, Rpb2 = cproj(1024 + q * 128, 2)
                    op("dve", lambda h, q=q, pb2=pb2, Csb=Csb: h.tensor_tensor(out=ubuf[:, q, 0:2], in0=Csb[:, 0:2], in1=pb2[:, 0:2], op=ALU.mult), reads=[RCsb, Rpb2], writes=[Rubuf])
                rmsnorm(nbufs, lambda c: hres[:, c, 0:TB], lambda c: hid[:, c, :], GMIX, TB, Rht[0], Rhid)
                for t in range(NTB):
                    cs = slice(t * TB, (t + 1) * TB)
                    Rh = Rht[t]

                    def conv_tail(q):
                        Bsb, RBsb = Bs[q % 2]
                        pb4, Rpb4 = bank()
                        for j in range(3):
                            op("pe", lambda h, j=j: h.matmul(pb4[:, :], lhsT=dg[:, j * 4 + q, :], rhs=ubuf[:, q, j:j + TB], start=(j == 0), stop=(j == 2)),
                               reads=[Rdg, Rubuf], writes=[Rpb4], inc=(j == 2))
                        op("dve", lambda h: h.tensor_tensor(out=ycv[:, q, :], in0=Bsb[:], in1=pb4[:, :], op=ALU.mult), reads=[RBsb, Rpb4], writes=[Rycv])
                        op("pool", lambda h: h.tensor_copy(out=ubuf[:, q, 0:2], in_=ubuf[:, q, TB:TB + 2]), reads=[Rubuf], writes=[Rubuf])

                    for q in range(4):
                        Csb, RCsb = Cs[q % 2]
                        Bsb, RBsb = Bs[q % 2]
                        pb, Rpb = cproj(512 + q * 128, TB)
                        op("act", lambda h, pb=pb, Csb=Csb: h.activation(out=Csb[:], in_=pb[:, :], func=AF.Copy), reads=[Rpb], writes=[RCsb])
                        pb2, Rpb2 = cproj(1024 + q * 128, TB)
                        op("dve", lambda h, q=q, pb2=pb2, Csb=Csb: h.tensor_tensor(out=ubuf[:, q, 2:TB + 2], in0=Csb[:], in1=pb2[:, :], op=ALU.mult), reads=[RCsb, Rpb2], writes=[Rubuf])
                        pb3, Rpb3 = cproj(q * 128, TB)
                        op("act", lambda h, pb3=pb3, Bsb=Bsb: h.activation(out=Bsb[:], in_=pb3[:, :], func=AF.Copy), reads=[Rpb3], writes=[RBsb])
                        if q >= 1:
                            conv_tail(q - 1)
                    conv_tail(3)
                    ncs = slice((t + 1) * TB, (t + 2) * TB)
                    pcs = slice((t - 1) * TB, t * TB)
                    if t + 1 < NTB:
                        rms_sq(nbufs, lambda c, ncs=ncs: hres[:, c, ncs], TB, Rht[t + 1])
                    for dc in range(8):
                        if dc == 3 and t + 1 < NTB:
                            rms_fin(nbufs, lambda c, ncs=ncs: hres[:, c, ncs], lambda c: hid[:, c, :], GMIX, TB, Rht[t + 1], Rhid)
                        if dc == 4 and t >= 1:
                            rms_sq(nbufs, lambda c, pcs=pcs: hres[:, c, pcs], TB, Rht[t - 1])
                        if dc == 7 and t >= 1:
                            rms_fin(nbufs, lambda c, pcs=pcs: hres[:, c, pcs], lambda c, pcs=pcs: xnm[:, c, pcs], GMLP, TB, Rht[t - 1], Rxnm)
                        pb, Rpb = bank()
                        for c in range(8):
                            r_ap = ycv[:, c, :] if c < 4 else yrw[:, c - 4, cs]
                            op("pe", lambda h, c=c, dc=dc, pb=pb, r_ap=r_ap: h.matmul(pb[:, :], lhsT=wo[:, c, dc * 128:(dc + 1) * 128], rhs=r_ap, start=(c == 0), stop=(c == 7)),
                               reads=[Rwo, Rycv, Ryrw], writes=[Rpb], inc=(c == 7))
                        op("dve", lambda h, dc=dc, cs=cs, pb=pb: h.tensor_tensor(out=hres[:, dc, cs], in0=hres[:, dc, cs], in1=pb[:, :], op=ALU.add), reads=[Rh, Rpb], writes=[Rh])
                S.barrier()
                if debug:
                    op("sp", lambda h: h.dma_start(out=dbg_h, in_=hres[:].rearrange("p a b -> p (a b)")), reads=Rht, dma=True)
                    S.barrier()
            rls = [sbt(sb_, "rl%d" % i, [128, TB], F32) for i in range(3)]
            rlc = [0]

            def nextrl():
                r = rls[rlc[0] % 3]
                rlc[0] += 1
                return r
            wpp, Rwpp = sbt(sb_, "wpp", [128, 2, 1024], BF16)
            pbfs = [sbt(sb_, "pbf%d" % i, [128, 2, TB], BF16) for i in range(2)]
            op("pool", lambda h: h.dma_start(out=wpp[:], in_=w_pp_d.rearrange("(c p) n -> p c n", p=128), max_dma_last_dim=4096), writes=[Rwpp], dma=True)
            for t in range(NTB - 1, NTB):
                cs = slice(t * TB, (t + 1) * TB)
                rmsnorm(nbufs, lambda c, cs=cs: hres[:, c, cs], lambda c, cs=cs: xnm[:, c, cs], GMLP, TB, Rht[t], Rxnm)

            def load_q(q):
                w1, Rw1 = wu[q % 2]
                w2, Rw2 = wd[q % 2]
                op("pool", lambda h: h.dma_start(out=w1, in_=w_up_d[:, q * 512:(q + 1) * 512].rearrange("(c p) n -> p c n", p=128), max_dma_last_dim=2048), writes=[Rw1], dma=True)
                op("pool", lambda h: h.dma_start(out=w2, in_=w_dn_d[q * 512:(q + 1) * 512, :].rearrange("(c p) n -> p c n", p=128), max_dma_last_dim=4096), writes=[Rw2], dma=True)

            load_q(0)
            wg, Rwg = wsq
            op("pool", lambda h: h.dma_start(out=wg[:], in_=w_gt_d.rearrange("(c p) n -> p c n", p=128), max_dma_last_dim=4096), writes=[Rwg], dma=True)
            for t in range(2):
                op("pool", lambda h, t=t: h.dma_start(out=pbfs[t][0][:], in_=pT[:, t * TB:(t + 1) * TB].rearrange("(c p) t -> p c t", p=128), max_dma_last_dim=2048), writes=[pbfs[t][1]], dma=True)
            load_q(1)
            iters = [(q, t) for q in range(8) for t in range(NTB)]
            hbuf = [(hid[:, 0:4, :], Res("hidA")), (hid[:, 4:8, :], Res("hidB2"))]

            def emit_up(k):
                q, t = iters[k]
                w1, Rw1 = wu[q % 2]
                hb, Rhb_ = hbuf[k % 2]
                cs = slice(t * TB, (t + 1) * TB)
                for fc in range(4):
                    pb, Rpb = bank()
                    rl, Rrl = nextrl()
                    for c in range(8):
                        op("pe", lambda h, c=c, fc=fc, cs=cs, pb=pb: h.matmul(pb[:, :], lhsT=w1[:, c, fc * 128:(fc + 1) * 128], rhs=xnm[:, c, cs], start=(c == 0), stop=(c == 7)),
                           reads=[Rw1, Rxnm], writes=[Rpb], inc=(c == 7))
                    op("act", lambda h, pb=pb, rl=rl: h.activation(out=rl[:], in_=pb[:, :], func=AF.Relu), reads=[Rpb], writes=[Rrl])
                    op("dve", lambda h, fc=fc, pb=pb, rl=rl: h.tensor_tensor(out=hb[:, fc, :], in0=rl[:], in1=pb[:, :], op=ALU.mult), reads=[Rrl, Rpb], writes=[Rhb_])

            def emit_down(k):
                q, t = iters[k]
                w2, Rw2 = wd[q % 2]
                hb, Rhb_ = hbuf[k % 2]
                cs = slice(t * TB, (t + 1) * TB)
                for dc in range(8):
                    pb, Rpb = bank()
                    for fc in range(4):
                        op("pe", lambda h, fc=fc, dc=dc, pb=pb: h.matmul(pb[:, :], lhsT=w2[:, fc, dc * 128:(dc + 1) * 128], rhs=hb[:, fc, :], start=(fc == 0), stop=(fc == 3)),
                           reads=[Rw2, Rhb_], writes=[Rpb], inc=(fc == 3))
                    op("dve", lambda h, dc=dc, cs=cs, pb=pb: h.tensor_tensor(out=hres[:, dc, cs], in0=hres[:, dc, cs], in1=pb[:, :], op=ALU.add), reads=[Rht[t], Rpb], writes=[Rht[t]])

            emit_up(0)
            for k in range(1, len(iters)):
                emit_up(k)
                emit_down(k - 1)
                q, t = iters[k]
                if t == 0 and q >= 1 and q + 1 < 8:
                    load_q(q + 1)
            emit_down(len(iters) - 1)
            S.barrier()
            obuf = big32[:, 0:8192].bitcast(F32).rearrange("p (c t) -> p c t", c=8)
            Robuf = Res("obuf")
            hidB = big32[:, 8192:12288].rearrange("p (c t) -> p c t", c=8)
            hids = [(hid[:], Rhid), (hidB, Res("hidB"))]
            sqF = big32[:, 12288:16384].rearrange("p (c t) -> p c t", c=8)
            lnvF, RlnvF = sbt(sb_, "lnvF", [128, TB], F32)
            rstdF, RrstdF = sbt(sb_, "rstdF", [128, TB], F32)
            nbufsF = (sqF, Res("sqF"), lnvF, RlnvF, rstdF, RrstdF)

            def ple_norm_a(t):
                pbf, Rpbf = pbfs[t % 2]
                if t >= 2:
                    op("pool", lambda h: h.dma_start(out=pbf[:], in_=pT[:, t * TB:(t + 1) * TB].rearrange("(c p) t -> p c t", p=128), max_dma_last_dim=2048), writes=[Rpbf], dma=True)
                rms_sq(nbufs, lambda c: hres[:, c, t * TB:(t + 1) * TB], TB, Rht[t])

            def ple_norm_b(t):
                hd, Rhd = hids[t % 2]
                rms_fin(nbufs, lambda c: hres[:, c, t * TB:(t + 1) * TB], lambda c: hd[:, c, :], GPLE, TB, Rht[t], Rhd)

            def fin_a(t):
                rms_sq(nbufsF, lambda c: hres[:, c, t * TB:(t + 1) * TB], TB, Rht[t])

            def fin_b(t):
                tcs = slice(t * TB, (t + 1) * TB)
                rms_fin(nbufsF, lambda c: hres[:, c, tcs], lambda c: obuf[:, c, :], GFIN, TB, Rht[t], Robuf)
                for hc in range(2):
                    op("sp", lambda h, hc=hc: h.dma_start(out=outT[hc * 512:(hc + 1) * 512, tcs].rearrange("(c p) t -> p c t", p=128), in_=obuf[:, hc * 4:(hc + 1) * 4, :]), reads=[Robuf], writes=[], dma=True)

            ple_norm_a(0)
            ple_norm_b(0)
            for t in range(NTB):
                cs = slice(t * TB, (t + 1) * TB)
                pbf, Rpbf = pbfs[t % 2]
                hd, Rhd = hids[t % 2]
                for dc in range(8):
                    rl, Rrl = nextrl()
                    pb, Rpb = bank()
                    for c in range(8):
                        op("pe", lambda h, c=c, dc=dc, pb=pb: h.matmul(pb[:, :], lhsT=wg[:, c, dc * 128:(dc + 1) * 128], rhs=hd[:, c, :], start=(c == 0), stop=(c == 7)),
                           reads=[Rwg, Rhd], writes=[Rpb], inc=(c == 7))
                    op("act", lambda h, pb=pb, rl=rl: h.activation(out=rl[:], in_=pb[:, :], func=AF.Sigmoid), reads=[Rpb], writes=[Rrl])
                    pb2, Rpb2 = bank()
                    for c in range(2):
                        op("pe", lambda h, c=c, dc=dc, pb2=pb2, pbf=pbf: h.matmul(pb2[:, :], lhsT=wpp[:, c, dc * 128:(dc + 1) * 128], rhs=pbf[:, c, :], start=(c == 0), stop=(c == 1)),
                           reads=[Rwpp, Rpbf], writes=[Rpb2], inc=(c == 1))
                    op("dve", lambda h, pb2=pb2, rl=rl: h.tensor_tensor(out=rl[:], in0=rl[:], in1=pb2[:, :], op=ALU.mult), reads=[Rrl, Rpb2], writes=[Rrl])
                    eng = "pool" if dc % 2 == 0 else "dve"
                    op(eng, lambda h, dc=dc, cs=cs, rl=rl: h.tensor_tensor(out=hres[:, dc, cs], in0=hres[:, dc, cs], in1=rl[:], op=ALU.add), reads=[Rht[t], Rrl], writes=[Rht[t]])
                    if dc == 0 and t + 1 < NTB:
                        ple_norm_a(t + 1)
                    if dc == 2 and t + 1 < NTB:
                        ple_norm_b(t + 1)
                    if dc == 3 and t >= 1:
                        fin_a(t - 1)
                    if dc == 5 and t >= 1:
                        fin_b(t - 1)
            fin_a(NTB - 1)
            fin_b(NTB - 1)
            S.barrier()
    return nc


_NC = [None]


def _host_consts():
    cst = np.zeros((128, NCC), np.float32)
    cst[:, CI:CI + 128] = np.eye(128, dtype=np.float32)
    ob = np.zeros((128, 128), np.float32)
    ob[0:64, 0:64] = 1.0
    ob[64:128, 64:128] = 1.0
    cst[:, COB:COB + 128] = ob
    cst[:, CON:CON + 128] = 1.0
    def bd(m):
        o = np.zeros((128, 128), np.float32)
        o[0:64, 0:64] = m
        o[64:128, 64:128] = m
        return o
    o64 = np.ones((64, 64), np.float32)
    cst[:, CML:CML + 512] = np.tile(bd(np.tril(o64, -1)), (1, 4))
    cst[:, CMT:CMT + 512] = np.tile(bd(np.triu(o64, 1)), (1, 4))
    cst[:, CMI:CMI + 512] = np.tile(bd(np.triu(o64, 0)), (1, 4))
    cst[:, CI8:CI8 + 512] = np.tile(np.eye(128, dtype=np.float32), (1, 4))
    smk = np.ones((128, 4 * TA), np.float32)
    smk[:, ::C] = 0.0
    return cst, smk


def kernel(x, p, norm_mix_g, w_in, conv_w, shift_mu, w_lora_up, w0, a_lora_up, a0, g_lora_up, k_k, k_a, r_k,
           ln_x_g, ln_x_b, w_out, norm_mlp_g, w_up, w_down, norm_ple_g, w_ple_gate, w_ple_proj, norm_final_g):
    f = lambda a: np.ascontiguousarray(np.asarray(a, dtype=np.float32))
    x = f(x)
    p = f(p)

    def cols(v, n):
        return f(v).reshape(n, 128).T

    prm = np.zeros((128, NP), np.float32)
    prm[:, GMIX:GMIX + 8] = cols(norm_mix_g[0], 8)
    prm[:, GMLP:GMLP + 8] = cols(norm_mlp_g[0], 8)
    prm[:, GPLE:GPLE + 8] = cols(norm_ple_g[0], 8)
    prm[:, GFIN:GFIN + 8] = cols(norm_final_g, 8)
    mu = f(shift_mu[0])
    prm[:, MU:MU + 13] = cols(mu[0:1664], 13)
    prm[:, MU + 13] = mu[1664:1792]
    prm[0:32, MU + 14] = mu[1792:1824]
    cw = f(conv_w[0])
    for j in range(3):
        prm[:, CW + j * 4:CW + j * 4 + 4] = cols(cw[j], 4)
    prm[:, W0:W0 + 4] = cols(w0[0], 4)
    prm[:, A0:A0 + 4] = cols(a0[0], 4)
    prm[:, KK:KK + 4] = cols(k_k[0], 4)
    prm[:, KA:KA + 4] = cols(k_a[0], 4)
    prm[:, RK:RK + 4] = cols(f(r_k[0]).reshape(-1), 4)
    prm[:, LG:LG + 4] = cols(ln_x_g[0], 4)
    prm[:, LB:LB + 4] = cols(ln_x_b[0], 4)
    cst, smk = _host_consts()
    lora = np.concatenate([f(w_lora_up[0]), f(a_lora_up[0])], axis=0)
    shared = {
        "prm": prm, "cst": cst, "smk": smk, "w_in": f(w_in[0]), "lora": f(lora), "glora": f(g_lora_up[0]),
        "w_out": f(w_out[0]), "w_up": f(w_up[0]), "w_down": f(w_down[0]), "w_gate": f(w_ple_gate[0]), "w_pproj": f(w_ple_proj[0]),
    }
    in_maps = []
    for c in range(8):
        b, half = c // 2, c % 2
        xo = x[b, half * NTOK:(half + 1) * NTOK, :].T
        xp = x[b, 0:NTOK, :].T if half == 1 else np.zeros((D, NTOK), np.float32)
        m = dict(shared)
        m["xT"] = f(np.concatenate([xp, xo], axis=1))
        m["pT"] = f(p[0, b, half * NTOK:(half + 1) * NTOK, :].T)
        in_maps.append(m)
    if _NC[0] is None:
        _NC[0] = build_program()
    res = run_bass_kernel_spmd(_NC[0], in_maps, core_ids=list(range(8)))
    out = np.zeros((4, 4096, D), np.float32)
    for c in range(8):
        b, half = c // 2, c % 2
        out[b, half * NTOK:(half + 1) * NTOK, :] = res.results[c]["outT"].T
    return out
```
